# Optimizing a Trainium2 kernel written in Bass

```python
import jax
import jax.numpy as jnp
from jax import lax
import numpy as np

D_MODEL = 1024
BATCH = 8
SEQ = 2048
DEPTH = 4
DEC_BATCH = 128
DEC_SEQ = 8
PAST_LEN = 16384
PAGE_SIZE = 128

N_META = 16
D_POOL = D_MODEL
POOL_WINDOWS = (2, 4, 8, 16)
N_POOL_GROUPS = len(POOL_WINDOWS)
POOL_GROUP_DIM = D_POOL // N_POOL_GROUPS
POOL_BUF = max(POOL_WINDOWS) - 1
D_INNER = 2 * D_MODEL
SSD_HEAD_DIM = 64
SSD_HEADS = D_INNER // SSD_HEAD_DIM
SSD_GROUPS = 4
HEADS_PER_GROUP = SSD_HEADS // SSD_GROUPS
D_STATE = 128
CONV_WIDTH = 4
CONV_DIM = D_INNER + 2 * SSD_GROUPS * D_STATE
CHUNK = 128
D_FF = 2816
FFN_CONV_WIDTH = 3
EPS = 1e-6
OFF_U = 0
OFF_Z = OFF_U + D_POOL
OFF_XBC = OFF_Z + D_INNER
OFF_DT = OFF_XBC + CONV_DIM
OFF_GA = OFF_DT + SSD_HEADS
OFF_GB = OFF_GA + D_MODEL
D_IN_TOTAL = OFF_GB + D_MODEL

kernel_name = 'hybrid_pool_ssd_convffn_step'


def rmsnorm(x, w):
    xf = x.astype(jnp.float32)
    y = xf * lax.rsqrt(jnp.mean(xf * xf, axis=-1, keepdims=True) + EPS)
    return (y * w.astype(jnp.float32)).astype(x.dtype)


def causal_dwconv(u, prefix, w, b):
    width = w.shape[0]
    L = u.shape[1]
    ext = jnp.concatenate([prefix.astype(u.dtype), u], axis=1)
    out = b
    for k in range(width):
        out = out + ext[:, k:k + L] * w[k]
    return out, ext[:, -(width - 1):]


def pool_mix(u, prefix, start_pos, pool_w, pool_scale):
    b, L, _ = u.shape
    ext = jnp.concatenate([prefix.astype(u.dtype), u], axis=1)
    cs = jnp.concatenate([jnp.zeros((b, 1, D_POOL), jnp.float32),
                          jnp.cumsum(ext.astype(jnp.float32), axis=1)], axis=1)
    pos = start_pos + jnp.arange(L, dtype=jnp.int32)
    uf = u.astype(jnp.float32)
    outs = []
    for g, win in enumerate(POOL_WINDOWS):
        c0, c1 = g * POOL_GROUP_DIM, (g + 1) * POOL_GROUP_DIM
        hi = cs[:, POOL_BUF + 1:POOL_BUF + 1 + L, c0:c1]
        lo = cs[:, POOL_BUF + 1 - win:POOL_BUF + 1 - win + L, c0:c1]
        count = jnp.minimum(pos + 1, win).astype(jnp.float32)[None, :, None]
        outs.append((hi - lo) / count - uf[:, :, c0:c1])
    d = jnp.stack(outs, axis=2).astype(u.dtype)
    mixed = jnp.einsum('blgc,gcd->blgd', d, pool_w).reshape(b, L, D_POOL) * pool_scale
    return mixed, ext[:, -POOL_BUF:]


def ssd_scan(xh, dt, a, bm, cm, h0):
    b, L = xh.shape[0], xh.shape[1]
    q = CHUNK if L % CHUNK == 0 else L
    nc = L // q

    def chunks(t):
        return jnp.moveaxis(t.reshape((b, nc, q) + t.shape[2:]), 1, 0)

    x_c = chunks(xh.astype(jnp.float32).reshape(b, L, SSD_GROUPS, HEADS_PER_GROUP, SSD_HEAD_DIM))
    dt_c = chunks(dt.reshape(b, L, SSD_GROUPS, HEADS_PER_GROUP))
    b_c = chunks(bm.astype(jnp.float32))
    c_c = chunks(cm.astype(jnp.float32))
    a_g = a.reshape(SSD_GROUPS, HEADS_PER_GROUP)
    causal = jnp.tril(jnp.ones((q, q), dtype=bool))[None, :, :, None, None]

    def step(h, inp):
        x, d, bc, cc = inp
        cum = jnp.cumsum(d * a_g, axis=1)
        seg = cum[:, :, None] - cum[:, None]
        decay = jnp.where(causal, jnp.exp(jnp.where(causal, seg, 0.0)), 0.0)
        xdt = x * d[..., None]
        scores = jnp.einsum('btgn,bsgn->btsg', cc, bc)
        m = scores[..., None] * decay
        y = jnp.einsum('btsgh,bsghp->btghp', m, xdt)
        y = y + jnp.einsum('btgn,bghpn->btghp', cc, h) * jnp.exp(cum)[..., None]
        last = cum[:, -1]
        xw = xdt * jnp.exp(last[:, None] - cum)[..., None]
        h = h * jnp.exp(last)[..., None, None] + jnp.einsum('bsgn,bsghp->bghpn', bc, xw)
        return h, y

    h_init = h0.astype(jnp.float32).reshape(b, SSD_GROUPS, HEADS_PER_GROUP, SSD_HEAD_DIM, D_STATE)
    h, y = lax.scan(step, h_init, (x_c, dt_c, b_c, c_c))
    y = jnp.moveaxis(y, 0, 1).reshape(b, L, SSD_HEADS, SSD_HEAD_DIM)
    return y, h.reshape(b, SSD_HEADS, SSD_HEAD_DIM, D_STATE)


def ssd(xh, dt, a, bm, cm, h0, n_lead):
    if n_lead > 0:
        y0, h = ssd_scan(xh[:, :n_lead], dt[:, :n_lead], a, bm[:, :n_lead], cm[:, :n_lead], h0)
        y1, h = ssd_scan(xh[:, n_lead:], dt[:, n_lead:], a, bm[:, n_lead:], cm[:, n_lead:], h)
        return jnp.concatenate([y0, y1], axis=1), h
    return ssd_scan(xh, dt, a, bm, cm, h0)


def gated_rmsnorm(y, z, w):
    yf = y.astype(jnp.float32) * jax.nn.silu(z.astype(jnp.float32))
    yg = yf.reshape(yf.shape[:-1] + (SSD_GROUPS, D_INNER // SSD_GROUPS))
    yg = yg * lax.rsqrt(jnp.mean(yg * yg, axis=-1, keepdims=True) + EPS)
    return (yg.reshape(yf.shape) * w.astype(jnp.float32)).astype(z.dtype)


def hybrid_layer(x, n_lead, start_pos, pool_buf, conv_buf, ssm_h0, ffn_buf,
                 norm1_w, w_in, pool_w, pool_scale, w_pool_out, conv_w, conv_b,
                 dt_bias, a_log, d_skip, ssd_norm_w, w_ssd_out, w_o,
                 norm2_w, w_up, ffn_conv_w, ffn_conv_b, w_down):
    b, L, _ = x.shape
    hn = rmsnorm(x, norm1_w)
    proj = hn @ w_in
    u = proj[..., OFF_U:OFF_Z]
    z = proj[..., OFF_Z:OFF_XBC]
    xbc = proj[..., OFF_XBC:OFF_DT]
    dt_raw = proj[..., OFF_DT:OFF_GA]
    g_a = proj[..., OFF_GA:OFF_GB]
    g_b = proj[..., OFF_GB:D_IN_TOTAL]
    pooled, pool_new = pool_mix(u, pool_buf, start_pos, pool_w, pool_scale)
    a_out = pooled @ w_pool_out
    xbc_c, conv_new = causal_dwconv(xbc, conv_buf, conv_w, conv_b)
    xbc_c = jax.nn.silu(xbc_c)
    xs = xbc_c[..., :D_INNER]
    bm = xbc_c[..., D_INNER:D_INNER + SSD_GROUPS * D_STATE].reshape(b, L, SSD_GROUPS, D_STATE)
    cm = xbc_c[..., D_INNER + SSD_GROUPS * D_STATE:].reshape(b, L, SSD_GROUPS, D_STATE)
    dt = jax.nn.softplus(dt_raw.astype(jnp.float32) + dt_bias.astype(jnp.float32))
    a = -jnp.exp(a_log.astype(jnp.float32))
    xh = xs.reshape(b, L, SSD_HEADS, SSD_HEAD_DIM)
    y, h_new = ssd(xh, dt, a, bm, cm, ssm_h0, n_lead)
    y = y + d_skip.astype(jnp.float32)[:, None] * xh.astype(jnp.float32)
    y = gated_rmsnorm(y.reshape(b, L, D_INNER), z, ssd_norm_w).astype(x.dtype)
    b_out = y @ w_ssd_out
    merged = jax.nn.sigmoid(g_a) * a_out + jax.nn.sigmoid(g_b) * b_out
    x = x + merged @ w_o
    hn2 = rmsnorm(x, norm2_w)
    up, ffn_new = causal_dwconv(hn2 @ w_up, ffn_buf, ffn_conv_w, ffn_conv_b)
    x = x + (jax.nn.silu(up[..., :D_FF]) * up[..., D_FF:]) @ w_down
    return x, pool_new, conv_new, h_new, ffn_new


def setup_inputs(seed: int = 0) -> dict:
    key = jax.random.key(seed)
    ks = jax.random.split(key, 32)
    f32 = jnp.float32

    def nrm(k, shape, scale):
        return jax.random.normal(k, shape, f32) * scale

    dt_init = jnp.exp(jax.random.uniform(ks[12], (DEPTH, SSD_HEADS), f32, np.log(1e-3), np.log(1e-1)))
    dt_bias = dt_init + jnp.log(-jnp.expm1(-dt_init))
    return {
        'x_prompt': nrm(ks[0], (BATCH, SEQ, D_MODEL), 1.0),
        'x_sample': nrm(ks[1], (DEC_BATCH, DEC_SEQ, D_MODEL), 1.0),
        'state_pool': nrm(ks[2], (DEPTH, DEC_BATCH, POOL_BUF, D_POOL), 1.0),
        'state_conv': nrm(ks[3], (DEPTH, DEC_BATCH, CONV_WIDTH - 1, CONV_DIM), 1.0),
        'state_ssm': nrm(ks[4], (DEPTH, DEC_BATCH, SSD_HEADS, SSD_HEAD_DIM, D_STATE), 0.1),
        'state_ffn': nrm(ks[5], (DEPTH, DEC_BATCH, FFN_CONV_WIDTH - 1, 2 * D_FF), 1.0),
        'meta_tokens': nrm(ks[6], (N_META, D_MODEL), 1.0),
        'norm1_w': 1.0 + nrm(ks[7], (DEPTH, D_MODEL), 0.01),
        'w_in': nrm(ks[8], (DEPTH, D_MODEL, D_IN_TOTAL), D_MODEL ** -0.5),
        'pool_w': nrm(ks[9], (DEPTH, N_POOL_GROUPS, POOL_GROUP_DIM, POOL_GROUP_DIM), POOL_GROUP_DIM ** -0.5),
        'pool_scale': 1.0 + nrm(ks[10], (DEPTH, D_POOL), 0.01),
        'w_pool_out': nrm(ks[11], (DEPTH, D_POOL, D_MODEL), D_POOL ** -0.5),
        'conv_w': nrm(ks[13], (DEPTH, CONV_WIDTH, CONV_DIM), CONV_WIDTH ** -0.5),
        'conv_b': nrm(ks[14], (DEPTH, CONV_DIM), 0.01),
        'dt_bias': dt_bias,
        'a_log': jnp.log(jax.random.uniform(ks[15], (DEPTH, SSD_HEADS), f32, 1.0, 16.0)),
        'd_skip': 1.0 + nrm(ks[16], (DEPTH, SSD_HEADS), 0.1),
        'ssd_norm_w': 1.0 + nrm(ks[17], (DEPTH, D_INNER), 0.01),
        'w_ssd_out': nrm(ks[18], (DEPTH, D_INNER, D_MODEL), D_INNER ** -0.5),
        'w_o': nrm(ks[19], (DEPTH, D_MODEL, D_MODEL), D_MODEL ** -0.5),
        'norm2_w': 1.0 + nrm(ks[20], (DEPTH, D_MODEL), 0.01),
        'w_up': nrm(ks[21], (DEPTH, D_MODEL, 2 * D_FF), D_MODEL ** -0.5),
        'ffn_conv_w': nrm(ks[22], (DEPTH, FFN_CONV_WIDTH, 2 * D_FF), FFN_CONV_WIDTH ** -0.5),
        'ffn_conv_b': nrm(ks[23], (DEPTH, 2 * D_FF), 0.01),
        'w_down': nrm(ks[24], (DEPTH, D_FF, D_MODEL), D_FF ** -0.5),
        'final_norm_w': 1.0 + nrm(ks[25], (D_MODEL,), 0.01),
    }


def reference(x_prompt, x_sample, state_pool, state_conv, state_ssm, state_ffn,
              meta_tokens, norm1_w, w_in, pool_w, pool_scale, w_pool_out, conv_w, conv_b,
              dt_bias, a_log, d_skip, ssd_norm_w, w_ssd_out, w_o, norm2_w, w_up,
              ffn_conv_w, ffn_conv_b, w_down, final_norm_w):
    b_p = x_prompt.shape[0]
    dt_ = x_prompt.dtype
    meta = jnp.broadcast_to(meta_tokens.astype(dt_)[None], (b_p, N_META, D_MODEL))
    xp = jnp.concatenate([meta, x_prompt], axis=1)
    xs = x_sample
    pp, pc, ph, pf = [], [], [], []
    sp, sc, sh, sf = [], [], [], []
    for l in range(DEPTH):
        params = (norm1_w[l], w_in[l], pool_w[l], pool_scale[l], w_pool_out[l], conv_w[l], conv_b[l],
                  dt_bias[l], a_log[l], d_skip[l], ssd_norm_w[l], w_ssd_out[l], w_o[l],
                  norm2_w[l], w_up[l], ffn_conv_w[l], ffn_conv_b[l], w_down[l])
        xp, p_pool, p_conv, p_h, p_ffn = hybrid_layer(
            xp, N_META, 0,
            jnp.zeros((b_p, POOL_BUF, D_POOL), dt_),
            jnp.zeros((b_p, CONV_WIDTH - 1, CONV_DIM), dt_),
            jnp.zeros((b_p, SSD_HEADS, SSD_HEAD_DIM, D_STATE), jnp.float32),
            jnp.zeros((b_p, FFN_CONV_WIDTH - 1, 2 * D_FF), dt_),
            *params)
        xs, s_pool, s_conv, s_h, s_ffn = hybrid_layer(
            xs, 0, PAST_LEN, state_pool[l], state_conv[l], state_ssm[l], state_ffn[l], *params)
        pp.append(p_pool); pc.append(p_conv); ph.append(p_h); pf.append(p_ffn)
        sp.append(s_pool); sc.append(s_conv); sh.append(s_h); sf.append(s_ffn)
    y_prompt = rmsnorm(xp, final_norm_w)[:, N_META:]
    y_sample = rmsnorm(xs, final_norm_w)
    return (y_prompt, y_sample,
            jnp.stack(pp), jnp.stack(pc), jnp.stack(ph), jnp.stack(pf),
            jnp.stack(sp), jnp.stack(sc), jnp.stack(sh), jnp.stack(sf))
```

```python
import numpy as np
import concourse.bass as bass
import concourse.mybir as mybir
from concourse.bass_utils import run_bass_kernel_spmd

F32 = mybir.dt.float32
BF16 = mybir.dt.bfloat16
AF = mybir.ActivationFunctionType
ALU = mybir.AluOpType
AX = mybir.AxisListType

D = 1024
DEPTH = 4
SEQ = 2048
NMETA = 16
NSEQ = 16
LS = 8
DI = 2048
NH = 32
HP = 64
NG = 4
DS = 128
CD = 3072
DFF = 2816
DIN = 8224
OFF_U, OFF_Z, OFF_XBC, OFF_DT, OFF_GA, OFF_GB = 0, 1024, 3072, 6144, 6176, 7200
EPS = 1e-6
N_CORES = 8
SAME_ENGINE_SYNC = True
import os
STAGE = int(os.environ.get('KSTAGE', '99'))
KLAYERS = int(os.environ.get('KLAYERS', '4'))
KTILES = int(os.environ.get('KTILES', '5'))


class Dep:
    __slots__ = ("name", "w", "r")

    def __init__(self, name=""):
        self.name = name
        self.w = None
        self.r = {}


class DS_:
    def __init__(self, nc, name):
        self.name = name
        self.sem = nc.alloc_semaphore("ds_" + name)
        self.count = 0


class Buf:
    def __init__(self, t, name):
        self.t = t
        self.d = Dep(name)


class KB:
    def __init__(self):
        self.nc = bass.Bass("TRN2", target_bir_lowering=False)
        nc = self.nc
        self.eng = {}
        for name, h in (("PE", nc.tensor), ("ACT", nc.scalar), ("DVE", nc.vector),
                        ("POOL", nc.gpsimd), ("SP", nc.sync)):
            self.eng[name] = dict(h=h, sem=nc.alloc_semaphore("e_" + name), n=0, seen={})
        self.nbuf = 0

    def sb(self, shape, dt, name=None):
        self.nbuf += 1
        name = name or f"b{self.nbuf}"
        return Buf(self.nc.alloc_sbuf_tensor(name, list(shape), dt), name)

    def ps(self, name):
        return Buf(self.nc.alloc_psum_tensor(name, [128, 512], F32), name)

    def _waits(self, ename, reads, writes):
        E = self.eng[ename]
        need = {}

        def add(tok):
            if tok is None:
                return
            key, sem, val = tok
            if key == ename and (ename == "PE" or not SAME_ENGINE_SYNC):
                return
            if key not in need or need[key][2] < val:
                need[key] = tok
        for d in reads:
            add(d.w)
        for d in writes:
            add(d.w)
            for key, t in d.r.items():
                add(t)
        for key, (k2, sem, val) in need.items():
            if E["seen"].get(key, 0) >= val:
                continue
            E["h"].wait_ge(sem, val)
            E["seen"][key] = val

    def op(self, ename, fn, R=(), W=(), inc=True):
        E = self.eng[ename]
        self._waits(ename, R, W)
        ins = fn(E["h"])
        if inc:
            E["n"] += 1
            ins.then_inc(E["sem"], 1)
            pos = E["n"]
        else:
            pos = E["n"] + 1
        tok = (ename, E["sem"], pos)
        for d in R:
            d.r[ename] = tok
        for d in W:
            d.w = tok
            d.r = {}
        return ins

    def dma(self, q, out, in_, ds, R=(), W=(), slow=False):
        E = self.eng[q]
        self._waits(q, R, W)
        if slow:
            ins = E["h"].dma_start(out=out, in_=in_, allow_slow_non_contiguous=True)
        else:
            ins = E["h"].dma_start(out=out, in_=in_)
        ds.count += 16
        ins.then_inc(ds.sem, 16)
        tok = ("dma_" + ds.name, ds.sem, ds.count)
        for d in R:
            d.r[tok[0]] = tok
        for d in W:
            d.w = tok
            d.r = {}

    def wait_all(self, q, dss):
        E = self.eng[q]
        for ds in dss:
            if ds.count:
                E["h"].wait_ge(ds.sem, ds.count)


def build_program():
    k = KB()
    nc = k.nc

    def din(name, shape):
        return nc.dram_tensor(name, list(shape), F32, kind="ExternalInput").ap()

    def dout(name, shape):
        return nc.dram_tensor(name, list(shape), F32, kind="ExternalOutput").ap()

    x_prompt = din("x_prompt", [SEQ, D])
    x_sample = din("x_sample", [NSEQ * LS, D])
    state_pool = din("state_pool", [DEPTH, NSEQ * 15, D])
    state_conv = din("state_conv", [DEPTH, NSEQ * 3, CD])
    state_ssm = din("state_ssm", [DEPTH, NSEQ, NH, HP, DS])
    state_ffn = din("state_ffn", [DEPTH, NSEQ * 2, 2 * DFF])
    meta_tokens = din("meta_tokens", [NMETA, D])
    norm1_w = din("norm1_w", [DEPTH, D])
    w_in = din("w_in", [DEPTH, D, DIN])
    pool_w = din("pool_w", [DEPTH, 4, 256, 256])
    pool_scale = din("pool_scale", [DEPTH, D])
    w_pool_out = din("w_pool_out", [DEPTH, D, D])
    conv_w = din("conv_w", [DEPTH, 4, CD])
    conv_b = din("conv_b", [DEPTH, CD])
    dt_bias = din("dt_bias", [DEPTH, NH])
    a_log = din("a_log", [DEPTH, NH])
    d_skip = din("d_skip", [DEPTH, NH])
    ssd_norm_w = din("ssd_norm_w", [DEPTH, DI])
    w_ssd_out = din("w_ssd_out", [DEPTH, DI, D])
    w_o = din("w_o", [DEPTH, D, D])
    norm2_w = din("norm2_w", [DEPTH, D])
    w_up = din("w_up", [DEPTH, D, 2 * DFF])
    ffn_conv_w = din("ffn_conv_w", [DEPTH, 3, 2 * DFF])
    ffn_conv_b = din("ffn_conv_b", [DEPTH, 2 * DFF])
    w_down = din("w_down", [DEPTH, DFF, D])
    final_norm_w = din("final_norm_w", [D])

    y_prompt = dout("y_prompt", [SEQ, D])
    y_sample = dout("y_sample", [NSEQ * LS, D])
    p_pool = dout("p_pool", [DEPTH, 15, D])
    p_conv = dout("p_conv", [DEPTH, 3, CD])
    p_ssm = dout("p_ssm", [DEPTH, NH, HP, DS])
    p_ffn = dout("p_ffn", [DEPTH, 2, 2 * DFF])
    s_pool = dout("s_pool", [DEPTH, NSEQ * 15, D])
    s_conv = dout("s_conv", [DEPTH, NSEQ * 3, CD])
    s_ssm = dout("s_ssm", [DEPTH, NSEQ, NH, HP, DS])
    s_ffn = dout("s_ffn", [DEPTH, NSEQ * 2, 2 * DFF])
    ssm_scr = nc.dram_tensor("ssm_scr", [DEPTH, NH, HP, DS], F32, kind="Internal").ap()

    out_dss = []

    def new_ds(name, is_out=False):
        ds = DS_(nc, name)
        if is_out:
            out_dss.append(ds)
        return ds

    ident_f = k.sb([128, 128], F32, "ident_f")
    ident_b = k.sb([128, 128], BF16, "ident_b")
    ones_b = k.sb([128, 128], BF16, "ones_b")
    ones_f = k.sb([128, 128], F32, "ones_f")
    triT = k.sb([128, 128], F32, "triT")
    Umat = k.sb([128, 128], F32, "Umat")
    E2 = k.sb([32, 2, 64], F32, "E2")
    Pm = k.sb([32, 16], F32, "Pm")
    tmsk = k.sb([32, 16], F32, "tmsk")
    epsc = k.sb([128, 1], F32, "epsc")
    onec = k.sb([128, 1], F32, "onec")
    icnt = k.sb([128, 4, 16], F32, "icnt")
    CD_ = Dep("consts")

    def pool_op(fn, W=(CD_,), R=()):
        return k.op("POOL", fn, R=R, W=W)

    pool_op(lambda e: e.memset(ones_f.t[:], 1.0))
    pool_op(lambda e: e.memset(ident_f.t[:], 1.0))
    pool_op(lambda e: e.affine_select(out=ident_f.t[:], in_=ident_f.t[:], pattern=[[1, 128]],
                                      compare_op=ALU.is_ge, fill=0.0, base=0, channel_multiplier=-1), R=(CD_,))
    pool_op(lambda e: e.affine_select(out=ident_f.t[:], in_=ident_f.t[:], pattern=[[-1, 128]],
                                      compare_op=ALU.is_ge, fill=0.0, base=0, channel_multiplier=1), R=(CD_,))
    pool_op(lambda e: e.memset(triT.t[:], 1.0))
    pool_op(lambda e: e.affine_select(out=triT.t[:], in_=triT.t[:], pattern=[[1, 128]],
                                      compare_op=ALU.is_ge, fill=0.0, base=0, channel_multiplier=-1), R=(CD_,))
    pool_op(lambda e: e.memset(Umat.t[:], 1.0))
    pool_op(lambda e: e.affine_select(out=Umat.t[:], in_=Umat.t[:], pattern=[[-1, 128]],
                                      compare_op=ALU.is_gt, fill=0.0, base=0, channel_multiplier=1), R=(CD_,))
    par = k.sb([2, 2, 64], F32, "par")
    pool_op(lambda e: e.memset(par.t[:], 1.0))
    pool_op(lambda e: e.affine_select(out=par.t[:], in_=par.t[:], pattern=[[1, 2], [0, 64]],
                                      compare_op=ALU.is_ge, fill=0.0, base=0, channel_multiplier=-1), R=(CD_,))
    pool_op(lambda e: e.affine_select(out=par.t[:], in_=par.t[:], pattern=[[-1, 2], [0, 64]],
                                      compare_op=ALU.is_ge, fill=0.0, base=0, channel_multiplier=1), R=(CD_,))
    e2_scr = nc.dram_tensor("e2_scr", [2, 128], F32, kind="Internal").ap()
    e2_ds = DS_(nc, "e2")
    k.dma("SP", e2_scr, par.t[:].rearrange("r a p -> r (a p)"), e2_ds, R=(CD_,))
    k.eng["SP"]["h"].wait_ge(e2_ds.sem, e2_ds.count)
    for a in range(16):
        k.dma("SP", E2.t[2 * a:2 * a + 2, :, :].rearrange("r a p -> r (a p)"), e2_scr, e2_ds, W=(CD_,))
    pool_op(lambda e: e.memset(Pm.t[:], 1.0))
    pool_op(lambda e: e.affine_select(out=Pm.t[:], in_=Pm.t[:], pattern=[[-2, 16]],
                                      compare_op=ALU.is_ge, fill=0.0, base=0, channel_multiplier=1), R=(CD_,))
    pool_op(lambda e: e.affine_select(out=Pm.t[:], in_=Pm.t[:], pattern=[[2, 16]],
                                      compare_op=ALU.is_ge, fill=0.0, base=1, channel_multiplier=-1), R=(CD_,))
    pool_op(lambda e: e.memset(epsc.t[:], EPS))
    pool_op(lambda e: e.memset(onec.t[:], 1.0))
    for g in range(4):
        pool_op(lambda e: e.iota(icnt.t[:, g, :], [[1, 16]], base=1, channel_multiplier=0,
                                 allow_small_or_imprecise_dtypes=True))
    k.op("DVE", lambda e: e.tensor_copy(out=ident_b.t[:], in_=ident_f.t[:]), R=(CD_,), W=(CD_,))
    k.op("DVE", lambda e: e.tensor_copy(out=ones_b.t[:], in_=ones_f.t[:]), R=(CD_,), W=(CD_,))
    for g in range(4):
        k.op("DVE", lambda e: e.tensor_scalar_min(out=icnt.t[:, g, :], in0=icnt.t[:, g, :],
                                                  scalar1=float(2 ** (g + 1))), R=(CD_,), W=(CD_,))
    k.op("DVE", lambda e: e.reciprocal(out=icnt.t[:], in_=icnt.t[:]), R=(CD_,), W=(CD_,))

    banks = [k.ps(f"pb{i}") for i in range(8)]
    bstate = {"i": 0}

    def nb():
        b = banks[bstate["i"] % 8]
        bstate["i"] += 1
        return b

    TM = 512
    x = k.sb([128, 8, TM], F32, "x")
    hn = k.sb([128, 8, TM], BF16, "hn")
    dbuf = k.sb([128, 8, TM], BF16, "dbuf")
    R1 = k.sb([128, 24 * TM], BF16, "R1")
    R2 = k.sb([128, 16 * TM], BF16, "R2")
    sz_d, bc_d = Dep("sz"), Dep("bc")
    xs_deps = [Dep(f"xs{i}") for i in range(17)]
    scp_d = Dep("scp")
    dtT = k.sb([32, 2, TM], F32, "dtT")
    sg = k.sb([128, 16, TM], BF16, "sg")
    rstd = k.sb([128, TM], F32, "rstd")
    NTMP = 5
    tmpf = [k.sb([128, 544], F32, f"tmpf{i}") for i in range(NTMP)]
    tstate = {"i": 0}

    def ntmp():
        b = tmpf[tstate["i"] % NTMP]
        tstate["i"] += 1
        return b
    NEXT = 3
    extb = [k.sb([128, 544], BF16, f"extb{i}") for i in range(NEXT)]
    exs = {"i": 0}

    def next_ext():
        b = extb[exs["i"] % NEXT]
        exs["i"] += 1
        return b
    NDG = 8
    dgb = [k.sb([128, 128], BF16, f"dg{i}") for i in range(NDG)]
    dgs = {"i": 0}

    def next_dg():
        b = dgb[dgs["i"] % NDG]
        dgs["i"] += 1
        return b
    sqb = [k.sb([128, TM], BF16, f"sqb{i}") for i in range(2)]
    sqs = {"i": 0}

    def nsq():
        b = sqb[sqs["i"] % 2]
        sqs["i"] += 1
        return b
    NSLOT = 3
    wslots = [k.sb([128, 4096], BF16, f"wslot{i}") for i in range(NSLOT)]
    wds = [new_ds(f"w{i}") for i in range(NSLOT)]
    wst = {"i": 0}

    stg = [k.sb([128, 1024], F32, f"stg{i}") for i in range(2)]
    stg_ds = [new_ds(f"stg{i}", is_out=True) for i in range(2)]
    stgs = {"i": 0}

    def nstg():
        i = stgs["i"] % 2
        stgs["i"] += 1
        return stg[i], stg_ds[i]

    prm1 = k.sb([128, DEPTH, 120], F32, "prm1")
    prm2 = k.sb([128, DEPTH, 84], F32, "prm2")
    prm3 = k.sb([128, DEPTH, 132], F32, "prm3")
    fnw = k.sb([128, 8], F32, "fnw")
    dskc = k.sb([128, DEPTH, 16], F32, "dskc")
    hcol = k.sb([32, DEPTH, 4], F32, "hcol")
    a_bc = k.sb([128, DEPTH, 32], F32, "a_bc")
    PRM = Dep("params")
    prm_ds = new_ds("prm")

    def load_rows(dst_stage, r0, src2d, nrows, ds):
        k.dma("SP", dst_stage.t[r0:r0 + nrows, 0:128], src2d, ds, W=(dst_stage.d,))

    for l in range(DEPTH):
        s1, sd1 = nstg()
        load_rows(s1, 0, norm1_w[l].rearrange("(c p) -> c p", p=128), 8, sd1)
        load_rows(s1, 8, norm2_w[l].rearrange("(c p) -> c p", p=128), 8, sd1)
        load_rows(s1, 16, pool_scale[l].rearrange("(c p) -> c p", p=128), 8, sd1)
        load_rows(s1, 24, conv_w[l].rearrange("k (c p) -> (k c) p", p=128), 96, sd1)
        pb = nb()
        k.op("PE", lambda e: e.transpose(pb.t[:, 0:120], s1.t[0:120, 0:128], ident_f.t[0:120, 0:120]),
             R=(s1.d, CD_), W=(pb.d,))
        k.op("ACT", lambda e: e.copy(out=prm1.t[:, l, :], in_=pb.t[:, 0:120]), R=(pb.d,), W=(PRM,))
        s2, sd2 = nstg()
        load_rows(s2, 0, conv_b[l].rearrange("(c p) -> c p", p=128), 24, sd2)
        load_rows(s2, 24, ssd_norm_w[l].rearrange("(c p) -> c p", p=128), 16, sd2)
        load_rows(s2, 40, ffn_conv_b[l].rearrange("(c p) -> c p", p=128), 44, sd2)
        pb = nb()
        k.op("PE", lambda e: e.transpose(pb.t[:, 0:84], s2.t[0:84, 0:128], ident_f.t[0:84, 0:84]),
             R=(s2.d, CD_), W=(pb.d,))
        k.op("ACT", lambda e: e.copy(out=prm2.t[:, l, :], in_=pb.t[:, 0:84]), R=(pb.d,), W=(PRM,))
        s3, sd3 = nstg()
        fw = ffn_conv_w[l].rearrange("k (c p) -> (k c) p", p=128)
        load_rows(s3, 0, fw[0:128], 128, sd3)
        k.dma("SP", s3.t[0:4, 128:256], fw[128:132], sd3, W=(s3.d,))
        pb = nb()
        k.op("PE", lambda e: e.transpose(pb.t[:, 0:128], s3.t[0:128, 0:128], ident_f.t[:, :]),
             R=(s3.d, CD_), W=(pb.d,), inc=False)
        k.op("PE", lambda e: e.transpose(pb.t[:, 128:132], s3.t[0:4, 128:256], ident_f.t[0:4, 0:4]),
             R=(s3.d, CD_), W=(pb.d,))
        k.op("ACT", lambda e: e.copy(out=prm3.t[:, l, :], in_=pb.t[:, 0:132]), R=(pb.d,), W=(PRM,))
        k.dma("SP", hcol.t[:, l, 0:1], dt_bias[l].rearrange("(h o) -> h o", o=1), prm_ds, W=(PRM,), slow=True)
        k.dma("SP", hcol.t[:, l, 1:2], a_log[l].rearrange("(h o) -> h o", o=1), prm_ds, W=(PRM,), slow=True)
        k.dma("SP", hcol.t[:, l, 2:3], d_skip[l].rearrange("(h o) -> h o", o=1), prm_ds, W=(PRM,), slow=True)
        k.dma("SP", a_bc.t[:, l, :], a_log[l].partition_broadcast(128), prm_ds, W=(PRM,))
    sF, sdF = nstg()
    load_rows(sF, 0, final_norm_w.rearrange("(c p) -> c p", p=128), 8, sdF)
    pb = nb()
    k.op("PE", lambda e: e.transpose(pb.t[:, 0:8], sF.t[0:8, 0:128], ident_f.t[0:8, 0:8]),
         R=(sF.d, CD_), W=(pb.d,))
    k.op("ACT", lambda e: e.copy(out=fnw.t[:], in_=pb.t[:, 0:8]), R=(pb.d,), W=(PRM,))
    k.op("ACT", lambda e: e.activation(out=hcol.t[:, :, 1:2], in_=hcol.t[:, :, 1:2], func=AF.Exp), R=(PRM,), W=(PRM,))
    k.op("DVE", lambda e: e.tensor_scalar_mul(out=hcol.t[:, :, 1:2], in0=hcol.t[:, :, 1:2], scalar1=-1.0),
         R=(PRM,), W=(PRM,))
    k.op("ACT", lambda e: e.activation(out=a_bc.t[:], in_=a_bc.t[:], func=AF.Exp), R=(PRM,), W=(PRM,))
    k.op("DVE", lambda e: e.tensor_scalar_mul(out=a_bc.t[:], in0=a_bc.t[:], scalar1=-1.0), R=(PRM,), W=(PRM,))
    Ef = E2.t[:].rearrange("h a p -> h (a p)")
    tmsk_d = Dep("tmsk")

    def expand_heads(col_ap, col_deps, out_fn):
        k.op("DVE", lambda e: e.tensor_scalar_mul(out=tmsk.t[:], in0=Pm.t[:], scalar1=col_ap),
             R=tuple(col_deps) + (CD_,), W=(tmsk_d,))
        pb = nb()
        k.op("PE", lambda e: e.matmul(pb.t[:, 0:16], Ef, tmsk.t[:], start=True, stop=True), R=(tmsk_d, CD_), W=(pb.d,))
        out_fn(pb)

    for l in range(DEPTH):
        expand_heads(hcol.t[:, l, 2:3], (PRM,),
                     lambda pb: k.op("ACT", lambda e: e.copy(out=dskc.t[:, l, :], in_=pb.t[:, 0:16]), R=(pb.d,), W=(PRM,)))

    pc_pool = [k.sb([128, 8, 1, 15], F32, f"pcp{l}") for l in range(DEPTH)]
    pc_conv = [k.sb([128, 24, 1, 3], F32, f"pcc{l}") for l in range(DEPTH)]
    pc_ffn = [k.sb([128, 44, 1, 2], F32, f"pcf{l}") for l in range(DEPTH)]

    class View:
        def __init__(self, t, d):
            self.t, self.d = t, d
    TMS = NMETA + NSEQ * LS
    R1f = R1.t.bitcast(F32)
    R2f = R2.t.bitcast(F32)
    o1 = (24 * TMS) // 2
    sc_conv = View(R1f[:, o1:o1 + 24 * NSEQ * 3].rearrange("p (c s r) -> p c s r", c=24, s=NSEQ), sz_d)
    o2 = o1 + 24 * NSEQ * 3
    sc_ffn = View(R1f[:, o2:o2 + 44 * NSEQ * 2].rearrange("p (c s r) -> p c s r", c=44, s=NSEQ), sz_d)
    o3 = (16 * TMS) // 2
    sc_pool = View(R2f[:, o3:o3 + 8 * NSEQ * 15].rearrange("p (c s r) -> p c s r", c=8, s=NSEQ), scp_d)

    S = k.sb([128, 16, 128], F32, "S")
    Sb = k.sb([128, 16, 128], BF16, "Sb")
    S_ds = new_ds("S", is_out=True)
    S2_ds = new_ds("S2", is_out=True)
    xs_tok = k.sb([128, 32, 64], BF16, "xs_tok")
    B_tok = k.sb([128, 4, 128], BF16, "B_tok")
    xdt = k.sb([128, 32, 64], BF16, "xdt")
    xw = xs_tok
    dtk = k.sb([128, 2, 32], F32, "dtk")
    cumk = k.sb([128, 4, 32], F32, "cumk")
    eL = k.sb([128, 16], F32, "eL")
    totf = k.sb([32, 1], F32, "totf")
    smk = k.sb([128, 4, 128], F32, "smk")
    Rg = [k.sb([128, 8, 128], F32, f"Rg{i}") for i in range(2)]
    dec = Rg
    Mg = [k.sb([128, 8, 128], BF16, f"Mg{i}") for i in range(2)]
    hTg = [k.sb([128, 512], BF16, "hTg0")] * 2
    ytok = [k.sb([128, 512], F32, "ytok0")] * 2
    y_f = [k.sb([128, 16, 128], F32, "y_f0")] * 2
    rstdg = k.sb([128, 4, 128], F32, "rstdg")
    rot = {"i": 0}
    Rg_deps = [Dep(f"Rg_g{i}") for i in range(4)]
    Mg_deps = [Dep(f"Mg_g{i}") for i in range(4)]

    Shn = View(hn.t.bitcast(F32)[:].rearrange("p a b -> p (a b)").rearrange("p (j n) -> p j n", n=128), hn.d)
    print("SBUF bytes remaining per partition:", nc.sbuf_bytes_remaining)

    def wload(srcs):
        i = wst["i"] % NSLOT
        wst["i"] += 1
        slot, ds = wslots[i], wds[i]
        for (src, dstview) in srcs:
            k.dma("POOL", dstview(slot), src, ds, W=(slot.d,))
        return slot

    def rms_rstd(sq_chunks_fn, nchunks, T, dim, out_ap, out_dep, pbank=None, pslice=None):
        pb = pbank or nb()
        pv = pslice if pslice is not None else pb.t[:, 0:T]
        for c in range(nchunks):
            sq, sqd = sq_chunks_fn(c)
            k.op("PE", lambda e: e.matmul(pv, ones_b.t[:, :], sq, start=(c == 0), stop=(c == nchunks - 1)),
                 R=(sqd, CD_), W=(pb.d,), inc=True)
        k.op("ACT", lambda e: e.activation(out=out_ap, in_=pv, func=AF.Sqrt, bias=epsc.t[:, 0:1], scale=1.0 / dim),
             R=(pb.d, CD_), W=(out_dep,))
        k.op("DVE", lambda e: e.reciprocal(out=out_ap, in_=out_ap), R=(out_dep,), W=(out_dep,))

    def rmsnorm_to(dst, T, wcol_fn):
        def sqf(c):
            b = nsq()
            k.op("ACT", lambda e: e.activation(out=b.t[:, 0:T], in_=x.t[:, c, 0:T], func=AF.Square),
                 R=(x.d,), W=(b.d,))
            return b.t[:, 0:T], b.d
        rms_rstd(sqf, 8, T, float(D), rstd.t[:, 0:T], rstd.d)
        for c in range(8):
            k.op("DVE", lambda e: e.scalar_tensor_tensor(out=dst.t[:, c, 0:T], in0=x.t[:, c, 0:T], scalar=wcol_fn(c),
                                                         in1=rstd.t[:, 0:T], op0=ALU.mult, op1=ALU.mult),
                 R=(x.d, rstd.d, PRM), W=(dst.d,))

    def load_tokmajor_T(src2d, nrows, ncols, dst_fn, dst_dep):
        for c0 in range(0, ncols, 1024):
            w = min(1024, ncols - c0)
            st, sds = nstg()
            k.dma("SP", st.t[0:nrows, 0:w], src2d[:, c0:c0 + w], sds, W=(st.d,))
            for q0 in range(0, w, 512):
                qn = min(512, w - q0)
                pb = nb()
                nchk = qn // 128
                for cc in range(nchk):
                    k.op("PE", lambda e: e.transpose(pb.t[:, cc * 128:cc * 128 + nrows],
                                                     st.t[0:nrows, q0 + cc * 128:q0 + (cc + 1) * 128],
                                                     ident_f.t[0:nrows, 0:nrows]),
                         R=(st.d, CD_), W=(pb.d,), inc=(cc == nchk - 1))
                for cc in range(nchk):
                    ch = (c0 + q0) // 128 + cc
                    k.op("ACT", lambda e: e.copy(out=dst_fn(ch), in_=pb.t[:, cc * 128:cc * 128 + nrows]),
                         R=(pb.d,), W=(dst_dep,))

    def store_featmajor_T(src_fn, src_dep, nrows, ncols, dst2d, out_ds_unused=None):
        for c0 in range(0, ncols, 1024):
            w = min(1024, ncols - c0)
            st, sds = nstg()
            for q0 in range(0, w, 512):
                qn = min(512, w - q0)
                pb = nb()
                nchk = qn // 128
                for cc in range(nchk):
                    ch = (c0 + q0) // 128 + cc
                    k.op("PE", lambda e: e.transpose(pb.t[0:nrows, cc * 128:(cc + 1) * 128], src_fn(ch),
                                                     ident_f.t[:, :]),
                         R=(src_dep, CD_), W=(pb.d,), inc=(cc == nchk - 1))
                k.op("ACT", lambda e: e.copy(out=st.t[0:nrows, q0:q0 + qn], in_=pb.t[0:nrows, 0:qn]),
                     R=(pb.d,), W=(st.d,))
            k.dma("SP", dst2d[:, c0:c0 + w], st.t[0:nrows, 0:w], sds, R=(st.d,))

    class Seg:
        def __init__(self, kind, c0, nseq, L):
            self.kind, self.c0, self.nseq, self.L = kind, c0, nseq, L

    tiles = [dict(kind="MS", T=NMETA + NSEQ * LS, segs=[Seg("meta", 0, 1, NMETA), Seg("samp", NMETA, NSEQ, LS)],
                  chunks=[("meta", 0, NMETA, None)] + [("samp", NMETA + LS * j, LS, j) for j in range(NSEQ)])]
    for i in range(4):
        tiles.append(dict(kind="P", idx=i, T=512, segs=[Seg("prompt", 0, 1, 512)],
                          chunks=[("prompt", 128 * c, 128, None) for c in range(4)]))

    def carry_of(seg, l, which):
        if seg.kind == "samp":
            return {"pool": sc_pool, "conv": sc_conv, "ffn": sc_ffn}[which]
        return {"pool": pc_pool, "conv": pc_conv, "ffn": pc_ffn}[which][l]

    def evac_to_ext(pb, T, seg, PRE, carry, ch):
        eb = ntmp()
        n, L = seg.nseq, seg.L
        ev = eb.t[:, 0:n * (PRE + L)].rearrange("p (s l) -> p s l", l=PRE + L)
        k.op("ACT", lambda e: e.copy(out=ev[:, :, PRE:PRE + L],
                                     in_=pb.t[:, seg.c0:seg.c0 + n * L].rearrange("p (s l) -> p s l", l=L)),
             R=(pb.d,), W=(eb.d,))
        k.op("DVE", lambda e: e.tensor_copy(out=ev[:, :, 0:PRE], in_=carry.t[:, ch, :, :]),
             R=(carry.d,), W=(eb.d,))
        k.op("DVE", lambda e: e.tensor_copy(out=carry.t[:, ch, :, :], in_=ev[:, :, L:L + PRE]),
             R=(eb.d,), W=(carry.d,))
        return eb, ev

    def conv_taps(eb, ev, seg, ntaps, wcols, bcol):
        n, L = seg.nseq, seg.L
        acc = ntmp()
        av = acc.t[:, 0:n * L].rearrange("p (s l) -> p s l", l=L)
        k.op("DVE", lambda e: e.tensor_scalar(out=av, in0=ev[:, :, 0:L], scalar1=wcols[0], scalar2=bcol,
                                              op0=ALU.mult, op1=ALU.add), R=(eb.d, PRM), W=(acc.d,))
        for t in range(1, ntaps):
            k.op("DVE", lambda e: e.scalar_tensor_tensor(out=av, in0=ev[:, :, t:t + L], scalar=wcols[t], in1=av,
                                                         op0=ALU.mult, op1=ALU.add), R=(eb.d, acc.d, PRM), W=(acc.d,))
        return acc, av

    def conv_pe(pb, seg, PRE, carry, ch, wcols):
        n, L = seg.nseq, seg.L
        eb = next_ext()
        ev = eb.t[:, 0:n * (PRE + L)].rearrange("p (s l) -> p s l", l=PRE + L)
        pv = pb.t[:, seg.c0:seg.c0 + n * L].rearrange("p (s l) -> p s l", l=L)
        k.op("ACT", lambda e: e.copy(out=ev[:, :, PRE:PRE + L], in_=pv), R=(pb.d,), W=(eb.d,))
        k.op("DVE", lambda e: e.tensor_copy(out=ev[:, :, 0:PRE], in_=carry.t[:, ch, :, :]), R=(carry.d,), W=(eb.d,))
        k.op("DVE", lambda e: e.tensor_copy(out=carry.t[:, ch, :, :], in_=pv[:, :, L - PRE:L]), R=(pb.d,), W=(carry.d,))
        dgs_ = []
        for wc in wcols:
            dg = next_dg()
            k.op("ACT", lambda e: e.activation(out=dg.t[:, :], in_=ident_f.t[:, :], func=AF.Copy, scale=wc),
                 R=(CD_, PRM), W=(dg.d,))
            dgs_.append(dg)
        pc = nb()
        nt = len(wcols)
        if n == 1 or os.environ.get("K3D", "1") == "1":
            pcv3 = pc.t[:, 0:n * L].rearrange("p (s l) -> p s l", l=L)
            for t_ in range(nt):
                k.op("PE", lambda e: e.matmul(pcv3 if n > 1 else pc.t[:, 0:L], dgs_[t_].t[:, :],
                                              ev[:, :, t_:t_ + L] if n > 1 else ev[:, 0, t_:t_ + L],
                                              start=(t_ == 0), stop=(t_ == nt - 1)),
                     R=(dgs_[t_].d, eb.d), W=(pc.d,), inc=True)
        else:
            for s_ in range(n):
                for t_ in range(nt):
                    k.op("PE", lambda e: e.matmul(pc.t[:, s_ * L:(s_ + 1) * L], dgs_[t_].t[:, :], ev[:, s_, t_:t_ + L],
                                                  start=(t_ == 0), stop=(t_ == nt - 1)),
                         R=(dgs_[t_].d, eb.d), W=(pc.d,), inc=(t_ == nt - 1))
        return pc, pc.t[:, 0:n * L].rearrange("p (s l) -> p s l", l=L)

    for ti, tile in enumerate(tiles[:KTILES]):
        T = tile["T"]
        is_ms = tile["kind"] == "MS"
        last_tile = (ti == len(tiles) - 1)
        nchunks_ssd = len(tile["chunks"])
        szv = R1.t[:, 0:16 * T].rearrange("p (c t) -> p c t", t=T)
        bcv = R1.t[:, 16 * T:24 * T].rearrange("p (c t) -> p c t", t=T)
        gv = R1.t[:, 0:22 * T].rearrange("p (c t) -> p c t", t=T)
        xsv = R2.t[:, 0:16 * T].rearrange("p (c t) -> p c t", t=T)
        R1W = (sz_d, bc_d)

        if is_ms:
            blocks = [(0, NMETA, meta_tokens), (NMETA, 128, x_sample)]
        else:
            r0 = tile["idx"] * 512
            blocks = [(128 * b, 128, x_prompt[r0 + 128 * b:r0 + 128 * (b + 1), :]) for b in range(4)]
        for (c0, n, src) in blocks:
            load_tokmajor_T(src, n, D, lambda ch: x.t[:, ch, c0:c0 + n], x.d)

        for l in range(KLAYERS if STAGE >= 1 else 0):
            n1c = lambda c: prm1.t[:, l, c:c + 1]
            n2c = lambda c: prm1.t[:, l, 8 + c:9 + c]
            psc = lambda c: prm1.t[:, l, 16 + c:17 + c]
            cwc = lambda t_, c: prm1.t[:, l, 24 + t_ * 24 + c:25 + t_ * 24 + c]
            cbc = lambda c: prm2.t[:, l, c:c + 1]
            snc = lambda c: prm2.t[:, l, 24 + c:25 + c]
            fbc = lambda c: prm2.t[:, l, 40 + c:41 + c]
            fwc = lambda t_, c: prm3.t[:, l, t_ * 44 + c:t_ * 44 + c + 1]

            if is_ms:
                k.op("DVE", lambda e: e.memset(pc_pool[l].t[:], 0.0), W=(pc_pool[l].d,))
                k.op("DVE", lambda e: e.memset(pc_conv[l].t[:], 0.0), W=(pc_conv[l].d,))
                k.op("DVE", lambda e: e.memset(pc_ffn[l].t[:], 0.0), W=(pc_ffn[l].d,))
                for half in range(2):
                    load_tokmajor_T(state_pool[l, half * 120:(half + 1) * 120, :], 120, D,
                                    lambda ch: sc_pool.t[:, ch, half * 8:(half + 1) * 8, :].rearrange("p s r -> p (s r)"),
                                    sc_pool.d)
                load_tokmajor_T(state_conv[l], 48, CD,
                                lambda ch: sc_conv.t[:, ch, :, :].rearrange("p s r -> p (s r)"), sc_conv.d)
                load_tokmajor_T(state_ffn[l], 32, 2 * DFF,
                                lambda ch: sc_ffn.t[:, ch, :, :].rearrange("p s r -> p (s r)"), sc_ffn.d)

            rmsnorm_to(hn, T, n1c)

            def inproj_block(col0, ncols, evac):
                nblk = ncols
                slot = wload([(w_in[l, :, col0:col0 + ncols].rearrange("(kt p) n -> p kt n", p=128),
                               lambda s: s.t[:, 0:8 * ncols].rearrange("p (kt n) -> p kt n", n=ncols))])
                wv = slot.t[:, 0:8 * ncols].rearrange("p (kt n) -> p kt n", n=ncols)
                for o in range(0, ncols, 128):
                    m = min(128, ncols - o)
                    pb = nb()
                    for kt in range(8):
                        k.op("PE", lambda e: e.matmul(pb.t[0:m, 0:T], wv[:, kt, o:o + m], hn.t[:, kt, 0:T],
                                                      start=(kt == 0), stop=(kt == 7)),
                             R=(slot.d, hn.d), W=(pb.d,), inc=(kt == 7))
                    evac(pb, (col0 + o))

            def evac_u(pb, col):
                ch = (col - OFF_U) // 128
                g = ch // 2
                win = 2 ** (g + 1)
                for seg in tile["segs"]:
                    carry = carry_of(seg, l, "pool")
                    eb, ev = evac_to_ext(pb, T, seg, 15, carry, ch)
                    n, L = seg.nseq, seg.L
                    W_ = 15 + L
                    cur, curv = eb, ev
                    sh = 1
                    lo = 0
                    while sh < win:
                        lo += sh
                        nbuf_ = ntmp()
                        nv = nbuf_.t[:, 0:n * W_].rearrange("p (s l) -> p s l", l=W_)
                        cv = curv
                        k.op("DVE", lambda e: e.tensor_tensor(out=nv[:, :, lo:W_], in0=cv[:, :, lo:W_],
                                                              in1=cv[:, :, lo - sh:W_ - sh], op=ALU.add),
                             R=(cur.d,), W=(nbuf_.d,))
                        cur, curv = nbuf_, nv
                        sh *= 2
                    dv = dbuf.t[:, ch, seg.c0:seg.c0 + n * L].rearrange("p (s l) -> p s l", l=L)
                    if seg.kind == "meta":
                        k.op("DVE", lambda e: e.tensor_tensor(out=curv[:, :, 15:15 + L], in0=curv[:, :, 15:15 + L],
                                                              in1=icnt.t[:, g:g + 1, 0:L], op=ALU.mult),
                             R=(cur.d, CD_), W=(cur.d,))
                        k.op("DVE", lambda e: e.tensor_tensor(out=dv, in0=curv[:, :, 15:15 + L],
                                                              in1=ev[:, :, 15:15 + L], op=ALU.subtract),
                             R=(cur.d, eb.d), W=(dbuf.d,))
                    else:
                        k.op("DVE", lambda e: e.scalar_tensor_tensor(out=dv, in0=curv[:, :, 15:15 + L],
                                                                     scalar=1.0 / win, in1=ev[:, :, 15:15 + L],
                                                                     op0=ALU.mult, op1=ALU.subtract),
                             R=(cur.d, eb.d), W=(dbuf.d,))

            def evac_z(pb, col):
                ch = (col - OFF_Z) // 128
                k.op("ACT", lambda e: e.activation(out=szv[:, ch, 0:T], in_=pb.t[:, 0:T], func=AF.Silu),
                     R=(pb.d,), W=R1W)

            def evac_xbc(pb, col):
                ch = (col - OFF_XBC) // 128
                for seg in tile["segs"]:
                    carry = carry_of(seg, l, "conv")
                    pc, pcv = conv_pe(pb, seg, 3, carry, ch, [cwc(t_, ch) for t_ in range(4)])
                    n, L = seg.nseq, seg.L
                    if ch < 16:
                        dst, dd = xsv[:, ch, seg.c0:seg.c0 + n * L], tuple(xs_deps) + (scp_d,)
                    else:
                        dst, dd = bcv[:, ch - 16, seg.c0:seg.c0 + n * L], R1W
                    k.op("ACT", lambda e: e.activation(out=dst.rearrange("p (s l) -> p s l", l=L), in_=pcv, func=AF.Silu,
                                                       bias=cbc(ch), scale=1.0), R=(pc.d, PRM), W=tuple(dd))

            def evac_dt(pb, col):
                k.op("ACT", lambda e: e.activation(out=dtT.t[:, 0, 0:T], in_=pb.t[0:32, 0:T], func=AF.Exp,
                                                   bias=hcol.t[:, l, 0:1], scale=1.0), R=(pb.d, PRM), W=(dtT.d,))
                k.op("ACT", lambda e: e.activation(out=dtT.t[:, 0, 0:T], in_=dtT.t[:, 0, 0:T], func=AF.Ln,
                                                   bias=onec.t[0:32, 0:1], scale=1.0), R=(dtT.d, CD_), W=(dtT.d,))
                k.op("DVE", lambda e: e.tensor_scalar_mul(out=dtT.t[:, 1, 0:T], in0=dtT.t[:, 0, 0:T],
                                                          scalar1=hcol.t[:, l, 1:2]), R=(dtT.d, PRM), W=(dtT.d,))

            def evac_g(pb, col):
                ch = (col - OFF_GA) // 128
                k.op("ACT", lambda e: e.activation(out=sg.t[:, ch, 0:T], in_=pb.t[:, 0:T], func=AF.Sigmoid),
                     R=(pb.d,), W=(sg.d,))

            for b in range(2):
                inproj_block(OFF_U + 512 * b, 512, evac_u)
            for b in range(4):
                inproj_block(OFF_Z + 512 * b, 512, evac_z)
            for b in range(6):
                inproj_block(OFF_XBC + 512 * b, 512, evac_xbc)
            inproj_block(OFF_DT, 32, evac_dt)
            for b in range(4):
                inproj_block(OFF_GA + 512 * b, 512, evac_g)

            if STAGE < 2:
                continue
            slot = wload([(pool_w[l].rearrange("g (kt p) n -> p g kt n", p=128),
                           lambda s: s.t[:, 0:2048].rearrange("p (g kt n) -> p g kt n", g=4, kt=2))])
            pwv = slot.t[:, 0:2048].rearrange("p (g kt n) -> p g kt n", g=4, kt=2)
            mixed = hn
            for g in range(4):
                for o in range(2):
                    pb = nb()
                    for kt in range(2):
                        k.op("PE", lambda e: e.matmul(pb.t[:, 0:T], pwv[:, g, kt, o * 128:(o + 1) * 128],
                                                      dbuf.t[:, 2 * g + kt, 0:T], start=(kt == 0), stop=(kt == 1)),
                             R=(slot.d, dbuf.d), W=(pb.d,), inc=(kt == 1))
                    ch = 2 * g + o
                    k.op("ACT", lambda e: e.activation(out=mixed.t[:, ch, 0:T], in_=pb.t[:, 0:T], func=AF.Copy,
                                                       scale=psc(ch)), R=(pb.d, PRM), W=(mixed.d,))
            for b in range(2):
                slot = wload([(w_pool_out[l, :, 512 * b:512 * (b + 1)].rearrange("(kt p) n -> p kt n", p=128),
                               lambda s: s.t[:, 0:4096].rearrange("p (kt n) -> p kt n", n=512))])
                wv = slot.t[:, 0:4096].rearrange("p (kt n) -> p kt n", n=512)
                for o in range(4):
                    pb = nb()
                    for kt in range(8):
                        k.op("PE", lambda e: e.matmul(pb.t[:, 0:T], wv[:, kt, o * 128:(o + 1) * 128],
                                                      mixed.t[:, kt, 0:T], start=(kt == 0), stop=(kt == 7)),
                             R=(slot.d, mixed.d), W=(pb.d,), inc=(kt == 7))
                    ch = 4 * b + o
                    k.op("DVE", lambda e: e.tensor_tensor(out=dbuf.t[:, ch, 0:T], in0=pb.t[:, 0:T],
                                                          in1=sg.t[:, ch, 0:T], op=ALU.mult),
                         R=(pb.d, sg.d), W=(dbuf.d,))

            for ci, (ckind, c0, Q, sj) in enumerate(tile["chunks"] if STAGE >= 3 else []):
                Sc, Sc_ds = S, S_ds
                if ckind == "meta":
                    k.op("DVE", lambda e: e.memset(S.t[:], 0.0), W=(S.d,))
                    k.op("DVE", lambda e: e.memset(Sb.t[:], 0.0), W=(Sb.d,))
                elif ckind == "samp":
                    sbufs = [(Shn, S2_ds), (S, S_ds)]

                    def load_samp(j):
                        b_, ds_ = sbufs[j % 2]
                        sv_ = state_ssm[l, j].rearrange("(j two) p n -> two p j n", two=2)
                        for two in range(2):
                            k.dma("SP", b_.t[two * 64:(two + 1) * 64, :, :], sv_[two], ds_, W=(b_.d,))
                    if sj == 0:
                        load_samp(0)
                    if sj + 1 < NSEQ:
                        load_samp(sj + 1)
                    Sc, Sc_ds = sbufs[sj % 2]
                    k.op("ACT", lambda e: e.copy(out=Sb.t[:], in_=Sc.t[:]), R=(Sc.d,), W=(Sb.d,))
                elif ckind == "prompt" and ci == 0:
                    sv = ssm_scr[l].rearrange("(j two) p n -> two p j n", two=2)
                    for two in range(2):
                        k.dma("SP", S.t[two * 64:(two + 1) * 64, :, :], sv[two], S_ds, W=(S.d,))
                    k.op("ACT", lambda e: e.copy(out=Sb.t[:], in_=S.t[:]), R=(S.d,), W=(Sb.d,))
                xd = xs_deps[ci]
                for r in range(2):
                    pb = nb()
                    pbv = pb.t.bitcast(BF16)
                    for jj in range(8):
                        j = 8 * r + jj
                        k.op("PE", lambda e: e.transpose(pbv[0:Q, jj * 128:(jj + 1) * 128], xsv[:, j, c0:c0 + Q],
                                                         ident_b.t[:, :]), R=(xd, CD_), W=(pb.d,), inc=(jj == 7))
                    k.op("ACT", lambda e: e.copy(out=xs_tok.t[0:Q, 16 * r:16 * (r + 1), :].rearrange("q h p -> q (h p)"),
                                                 in_=pbv[0:Q, 0:1024]), R=(pb.d,), W=(xs_tok.d,))
                pb = nb()
                pbv = pb.t.bitcast(BF16)
                for g in range(4):
                    k.op("PE", lambda e: e.transpose(pbv[0:Q, g * 128:(g + 1) * 128], bcv[:, g, c0:c0 + Q],
                                                     ident_b.t[:, :]), R=(bc_d, CD_), W=(pb.d,), inc=(g == 3))
                k.op("ACT", lambda e: e.copy(out=B_tok.t[0:Q, :, :].rearrange("q g n -> q (g n)"), in_=pbv[0:Q, 0:512]),
                     R=(pb.d,), W=(B_tok.d,))
                pb = nb()
                for w_ in range(2):
                    k.op("PE", lambda e: e.transpose(pb.t[0:Q, 32 * w_:32 * (w_ + 1)], dtT.t[:, w_, c0:c0 + Q],
                                                     ident_f.t[0:32, 0:32]), R=(dtT.d, CD_), W=(pb.d,), inc=(w_ == 1))
                k.op("ACT", lambda e: e.copy(out=dtk.t[0:Q, :, :].rearrange("q a h -> q (a h)"), in_=pb.t[0:Q, 0:64]),
                     R=(pb.d,), W=(dtk.d,))
                pb = nb()
                k.op("PE", lambda e: e.matmul(pb.t[0:Q, 0:32], triT.t[0:Q, 0:Q], dtk.t[0:Q, 1, :], start=True, stop=True),
                     R=(dtk.d, CD_), W=(pb.d,), inc=False)
                k.op("PE", lambda e: e.matmul(pb.t[0:Q, 32:64], ones_f.t[0:Q, 0:Q], dtk.t[0:Q, 1, :], start=True, stop=True),
                     R=(dtk.d, CD_), W=(pb.d,))
                k.op("ACT", lambda e: e.copy(out=cumk.t[0:Q, 0, :], in_=pb.t[0:Q, 0:32]), R=(pb.d,), W=(cumk.d,))
                k.op("ACT", lambda e: e.activation(out=cumk.t[0:Q, 1, :], in_=pb.t[0:Q, 0:32], func=AF.Exp),
                     R=(pb.d,), W=(cumk.d,))
                k.op("DVE", lambda e: e.tensor_tensor(out=cumk.t[0:Q, 2, :], in0=pb.t[0:Q, 32:64], in1=cumk.t[0:Q, 0, :],
                                                      op=ALU.subtract), R=(pb.d, cumk.d), W=(cumk.d,))
                k.op("ACT", lambda e: e.activation(out=cumk.t[0:Q, 2, :], in_=cumk.t[0:Q, 2, :], func=AF.Exp),
                     R=(cumk.d,), W=(cumk.d,))
                k.op("DVE", lambda e: e.reduce_sum(out=totf.t[:, 0:1], in_=dtT.t[:, 1, c0:c0 + Q], axis=AX.X),
                     R=(dtT.d,), W=(totf.d,))
                expand_heads(totf.t[:, 0:1], (totf.d,),
                             lambda pb: k.op("ACT", lambda e: e.activation(out=eL.t[:, :], in_=pb.t[:, 0:16], func=AF.Exp),
                                             R=(pb.d,), W=(eL.d,)))
                pb = nb()
                for g in range(4):
                    k.op("PE", lambda e: e.matmul(pb.t[0:Q, g * 128:g * 128 + Q], bcv[:, g, c0:c0 + Q],
                                                  bcv[:, 4 + g, c0:c0 + Q], start=True, stop=True),
                         R=(bc_d,), W=(pb.d,), inc=(g == 3))
                k.op("DVE", lambda e: e.tensor_tensor(out=smk.t[0:Q, :, 0:Q],
                                                      in0=pb.t[0:Q, :].rearrange("q (g t) -> q g t", g=4)[:, :, 0:Q],
                                                      in1=triT.t[0:Q, 0:Q].unsqueeze(1).broadcast_to([Q, 4, Q]),
                                                      op=ALU.mult), R=(pb.d, CD_), W=(smk.d,))
                k.op("DVE", lambda e: e.tensor_tensor(out=xdt.t[0:Q, :, :], in0=xs_tok.t[0:Q, :, :],
                                                      in1=dtk.t[0:Q, 0, :].unsqueeze(2).broadcast_to([Q, 32, 64]),
                                                      op=ALU.mult), R=(xs_tok.d, dtk.d), W=(xdt.d,))
                k.op("DVE", lambda e: e.tensor_tensor(out=xw.t[0:Q, :, :], in0=xdt.t[0:Q, :, :],
                                                      in1=cumk.t[0:Q, 2, :].unsqueeze(2).broadcast_to([Q, 32, 64]),
                                                      op=ALU.mult), R=(xdt.d, cumk.d), W=(xw.d,))
                yf = y_f[rot["i"] % 2]
                rot["i"] += 1
                for g in range(4):
                    R_, dc, M_, hT, yt = Rg[g % 2], dec[g % 2], Mg[g % 2], hTg[g % 2], ytok[g % 2]
                    if Q <= 32:
                        go, Rd, Md = g * Q, Rg_deps[g], Mg_deps[g]
                    else:
                        go, Rd, Md = 0, R_.d, M_.d
                    k.op("DVE", lambda e: e.tensor_tensor(out=R_.t[0:Q, :, go:go + Q],
                                                          in0=dtk.t[0:Q, 1, 8 * g:8 * g + 8].unsqueeze(2).broadcast_to([Q, 8, Q]),
                                                          in1=triT.t[0:Q, 0:Q].unsqueeze(1).broadcast_to([Q, 8, Q]),
                                                          op=ALU.mult), R=(dtk.d, CD_), W=(Rd,))
                    hp = 4 if Q == 128 else 8
                    for hb in range(0, 8, hp):
                        pb = nb()
                        if Q == 128:
                            k.op("PE", lambda e: e.matmul(pb.t[0:Q, 0:512], Umat.t[0:Q, 0:Q],
                                                          R_.t[0:Q, hb:hb + 4, :].rearrange("q h t -> q (h t)"),
                                                          start=True, stop=True), R=(Rd, CD_), W=(pb.d,))
                            pv = pb.t[0:Q, 0:512].rearrange("q (h t) -> q h t", h=4)
                        else:
                            for hh in range(8):
                                k.op("PE", lambda e: e.matmul(pb.t[0:Q, hh * Q:(hh + 1) * Q], Umat.t[0:Q, 0:Q],
                                                              R_.t[0:Q, hh, go:go + Q], start=True, stop=True),
                                     R=(Rd, CD_), W=(pb.d,), inc=(hh == 7))
                            pv = pb.t[0:Q, 0:8 * Q].rearrange("q (h t) -> q h t", h=8)
                        k.op("ACT", lambda e: e.activation(out=dc.t[0:Q, hb:hb + hp, go:go + Q], in_=pv, func=AF.Exp),
                             R=(pb.d,), W=(Rd,))
                    k.op("DVE", lambda e: e.tensor_tensor(out=M_.t[0:Q, :, go:go + Q], in0=dc.t[0:Q, :, go:go + Q],
                                                          in1=smk.t[0:Q, g, 0:Q].unsqueeze(1).broadcast_to([Q, 8, Q]),
                                                          op=ALU.mult), R=(Rd, smk.d), W=(Md,))
                    pbY = nb()
                    for hh in range(8):
                        h = 8 * g + hh
                        k.op("PE", lambda e: e.matmul(pbY.t[0:Q, hh * 64:(hh + 1) * 64], M_.t[0:Q, hh, go:go + Q],
                                                      xdt.t[0:Q, h, :], start=True, stop=True),
                             R=(Md, xdt.d), W=(pbY.d,), inc=(hh == 7))
                    pbT = nb()
                    pbTv = pbT.t.bitcast(BF16)
                    for jj in range(4):
                        j = 4 * g + jj
                        k.op("PE", lambda e: e.transpose(pbTv[:, jj * 128:(jj + 1) * 128], Sb.t[:, j, :], ident_b.t[:, :]),
                             R=(Sb.d, CD_), W=(pbT.d,), inc=(jj == 3))
                    k.op("ACT", lambda e: e.copy(out=hT.t[:, :], in_=pbTv[:, 0:512]), R=(pbT.d,), W=(hT.d,))
                    pbC = nb()
                    k.op("PE", lambda e: e.matmul(pbC.t[0:Q, 0:512], bcv[:, 4 + g, c0:c0 + Q], hT.t[:, :],
                                                  start=True, stop=True), R=(bc_d, hT.d), W=(pbC.d,))
                    k.op("DVE", lambda e: e.tensor_tensor(out=yt.t[0:Q, :].rearrange("q (h p) -> q h p", h=8),
                                                          in0=pbC.t[0:Q, :].rearrange("q (h p) -> q h p", h=8),
                                                          in1=cumk.t[0:Q, 1, 8 * g:8 * g + 8].unsqueeze(2).broadcast_to([Q, 8, 64]),
                                                          op=ALU.mult), R=(pbC.d, cumk.d), W=(yt.d,))
                    k.op("DVE", lambda e: e.tensor_tensor(out=yt.t[0:Q, :], in0=yt.t[0:Q, :], in1=pbY.t[0:Q, 0:512],
                                                          op=ALU.add), R=(pbY.d, yt.d), W=(yt.d,))
                    pbF = nb()
                    for jj in range(4):
                        k.op("PE", lambda e: e.transpose(pbF.t[:, jj * 128:jj * 128 + Q], yt.t[0:Q, jj * 128:(jj + 1) * 128],
                                                         ident_f.t[0:Q, 0:Q]), R=(yt.d, CD_), W=(pbF.d,), inc=(jj == 3))
                    k.op("ACT", lambda e: e.copy(out=yf.t[:, 4 * g:4 * g + 4, 0:Q],
                                                 in_=pbF.t[:, :].rearrange("p (j t) -> p j t", j=4)[:, :, 0:Q]),
                         R=(pbF.d,), W=(yf.d,))
                    pbD = nb()
                    for jj in range(4):
                        j = 4 * g + jj
                        k.op("PE", lambda e: e.matmul(pbD.t[:, jj * 128:(jj + 1) * 128],
                                                      xw.t[0:Q, 2 * j:2 * j + 2, :].rearrange("q h p -> q (h p)"),
                                                      B_tok.t[0:Q, g, :], start=True, stop=True),
                             R=(xw.d, B_tok.d), W=(pbD.d,), inc=(jj == 3))
                    Sg = Sc.t[:, 4 * g:4 * g + 4, :]
                    k.op("DVE", lambda e: e.tensor_tensor(out=Sg, in0=Sg,
                                                          in1=eL.t[:, 4 * g:4 * g + 4].unsqueeze(2).broadcast_to([128, 4, 128]),
                                                          op=ALU.mult), R=(Sc.d, eL.d), W=(Sc.d,))
                    k.op("DVE", lambda e: e.tensor_tensor(out=Sg, in0=Sg,
                                                          in1=pbD.t[:, :].rearrange("p (j n) -> p j n", j=4),
                                                          op=ALU.add), R=(Sc.d, pbD.d), W=(Sc.d,))
                    if ckind != "samp":
                        k.op("ACT", lambda e: e.copy(out=Sb.t[:, 4 * g:4 * g + 4, :], in_=Sg), R=(Sc.d,), W=(Sb.d,))
                if ckind == "samp":
                    dv = s_ssm[l, sj].rearrange("(j two) p n -> two p j n", two=2)
                    for two in range(2):
                        k.dma("SP", dv[two], Sc.t[two * 64:(two + 1) * 64, :, :], Sc_ds, R=(Sc.d,))
                elif ckind == "meta" or ci == nchunks_ssd - 1:
                    dst = p_ssm[l] if (last_tile and ckind == "prompt") else ssm_scr[l]
                    dv = dst.rearrange("(j two) p n -> two p j n", two=2)
                    for two in range(2):
                        k.dma("SP", dv[two], S.t[two * 64:(two + 1) * 64, :, :], S_ds, R=(S.d,))
                k.op("DVE", lambda e: e.tensor_tensor(out=xsv[:, :, c0:c0 + Q], in0=xsv[:, :, c0:c0 + Q],
                                                      in1=dskc.t[:, l, :].unsqueeze(2).broadcast_to([128, 16, Q]),
                                                      op=ALU.mult), R=(xd, PRM), W=(xd,))
                k.op("DVE", lambda e: e.tensor_tensor(out=yf.t[:, :, 0:Q], in0=yf.t[:, :, 0:Q], in1=xsv[:, :, c0:c0 + Q],
                                                      op=ALU.add), R=(xd, yf.d), W=(yf.d,))
                k.op("DVE", lambda e: e.tensor_tensor(out=yf.t[:, :, 0:Q], in0=yf.t[:, :, 0:Q], in1=szv[:, :, c0:c0 + Q],
                                                      op=ALU.mult), R=(yf.d, sz_d), W=(yf.d,))
                pbN = nb()
                for g in range(4):
                    def sqf(c, g=g):
                        b = nsq()
                        k.op("ACT", lambda e: e.activation(out=b.t[:, 0:Q], in_=yf.t[:, 4 * g + c, 0:Q], func=AF.Square),
                             R=(yf.d,), W=(b.d,))
                        return b.t[:, 0:Q], b.d
                    rms_rstd(sqf, 4, Q, 512.0, rstdg.t[:, g, 0:Q], rstdg.d, pbank=pbN, pslice=pbN.t[:, g * 128:g * 128 + Q])
                yf4 = yf.t[:, :, :].rearrange("p (g j) t -> p g j t", g=4)[:, :, :, 0:Q]
                k.op("DVE", lambda e: e.tensor_tensor(out=yf4, in0=yf4,
                                                      in1=rstdg.t[:, :, 0:Q].unsqueeze(2).broadcast_to([128, 4, 4, Q]),
                                                      op=ALU.mult), R=(yf.d, rstdg.d), W=(yf.d,))
                k.op("DVE", lambda e: e.tensor_tensor(out=xsv[:, :, c0:c0 + Q], in0=yf.t[:, :, 0:Q],
                                                      in1=prm2.t[:, l, 24:40].unsqueeze(2).broadcast_to([128, 16, Q]),
                                                      op=ALU.mult), R=(yf.d, PRM), W=(xd,))

            if STAGE < 4:
                continue
            merged = dbuf
            for b in range(4):
                slot = wload([(w_ssd_out[l, :, 256 * b:256 * (b + 1)].rearrange("(kt p) n -> p kt n", p=128),
                               lambda s: s.t[:, 0:4096].rearrange("p (kt n) -> p kt n", n=256))])
                wv = slot.t[:, 0:4096].rearrange("p (kt n) -> p kt n", n=256)
                for o in range(2):
                    pb = nb()
                    for kt in range(16):
                        k.op("PE", lambda e: e.matmul(pb.t[:, 0:T], wv[:, kt, o * 128:(o + 1) * 128],
                                                      xsv[:, kt, 0:T], start=(kt == 0), stop=(kt == 15)),
                             R=tuple([slot.d] + xs_deps), W=(pb.d,), inc=(kt == 15))
                    ch = 2 * b + o
                    tb = ntmp()
                    k.op("DVE", lambda e: e.tensor_tensor(out=tb.t[:, 0:T], in0=pb.t[:, 0:T], in1=sg.t[:, 8 + ch, 0:T],
                                                          op=ALU.mult), R=(pb.d, sg.d), W=(tb.d,))
                    k.op("DVE", lambda e: e.tensor_tensor(out=merged.t[:, ch, 0:T], in0=tb.t[:, 0:T],
                                                          in1=dbuf.t[:, ch, 0:T], op=ALU.add),
                         R=(tb.d, dbuf.d), W=(merged.d,))
            for b in range(2):
                slot = wload([(w_o[l, :, 512 * b:512 * (b + 1)].rearrange("(kt p) n -> p kt n", p=128),
                               lambda s: s.t[:, 0:4096].rearrange("p (kt n) -> p kt n", n=512))])
                wv = slot.t[:, 0:4096].rearrange("p (kt n) -> p kt n", n=512)
                for o in range(4):
                    pb = nb()
                    for kt in range(8):
                        k.op("PE", lambda e: e.matmul(pb.t[:, 0:T], wv[:, kt, o * 128:(o + 1) * 128],
                                                      merged.t[:, kt, 0:T], start=(kt == 0), stop=(kt == 7)),
                             R=(slot.d, merged.d), W=(pb.d,), inc=(kt == 7))
                    ch = 4 * b + o
                    k.op("DVE", lambda e: e.tensor_tensor(out=x.t[:, ch, 0:T], in0=x.t[:, ch, 0:T], in1=pb.t[:, 0:T],
                                                          op=ALU.add), R=(pb.d, x.d), W=(x.d,))
            if STAGE < 5:
                continue
            rmsnorm_to(hn, T, n2c)
            for jb in range(11):
                slot = wload([(w_up[l, :, 256 * jb:256 * (jb + 1)].rearrange("(kt p) n -> p kt n", p=128),
                               lambda s: s.t[:, 0:4096].rearrange("p (kt n) -> p kt n", n=512)[:, :, 0:256]),
                              (w_up[l, :, DFF + 256 * jb:DFF + 256 * (jb + 1)].rearrange("(kt p) n -> p kt n", p=128),
                               lambda s: s.t[:, 0:4096].rearrange("p (kt n) -> p kt n", n=512)[:, :, 256:512])])
                wv = slot.t[:, 0:4096].rearrange("p (kt n) -> p kt n", n=512)
                for o in range(2):
                    ca = 2 * jb + o
                    pbs = []
                    for half in range(2):
                        pb = nb()
                        for kt in range(8):
                            k.op("PE", lambda e: e.matmul(pb.t[:, 0:T], wv[:, kt, half * 256 + o * 128:half * 256 + (o + 1) * 128],
                                                          hn.t[:, kt, 0:T], start=(kt == 0), stop=(kt == 7)),
                                 R=(slot.d, hn.d), W=(pb.d,), inc=(kt == 7))
                        pbs.append(pb)
                    for seg in tile["segs"]:
                        res = []
                        for half in range(2):
                            chf = ca + 22 * half
                            carry = carry_of(seg, l, "ffn")
                            res.append(conv_pe(pbs[half], seg, 2, carry, chf, [fwc(t_, chf) for t_ in range(3)]))
                        (pcA, pcAv), (pcB, pcBv) = res
                        n, L = seg.nseq, seg.L
                        ta = ntmp()
                        tav = ta.t[:, 0:n * L].rearrange("p (s l) -> p s l", l=L)
                        k.op("ACT", lambda e: e.activation(out=tav, in_=pcAv, func=AF.Silu, bias=fbc(ca), scale=1.0),
                             R=(pcA.d, PRM), W=(ta.d,))
                        k.op("DVE", lambda e: e.scalar_tensor_tensor(out=gv[:, ca, seg.c0:seg.c0 + n * L].rearrange("p (s l) -> p s l", l=L),
                                                                     in0=pcBv, scalar=fbc(ca + 22), in1=tav,
                                                                     op0=ALU.add, op1=ALU.mult),
                             R=(pcB.d, ta.d, PRM), W=R1W)
            for o in range(8):
                slot = wload([(w_down[l, :, 128 * o:128 * (o + 1)].rearrange("(kt p) n -> p kt n", p=128),
                               lambda s: s.t[:, 0:2816].rearrange("p (kt n) -> p kt n", n=128))])
                wv = slot.t[:, 0:2816].rearrange("p (kt n) -> p kt n", n=128)
                pb = nb()
                for kt in range(22):
                    k.op("PE", lambda e: e.matmul(pb.t[:, 0:T], wv[:, kt, :], gv[:, kt, 0:T],
                                                  start=(kt == 0), stop=(kt == 21)),
                         R=(slot.d, sz_d, bc_d), W=(pb.d,), inc=(kt == 21))
                k.op("DVE", lambda e: e.tensor_tensor(out=x.t[:, o, 0:T], in0=x.t[:, o, 0:T], in1=pb.t[:, 0:T],
                                                      op=ALU.add), R=(pb.d, x.d), W=(x.d,))

            if is_ms:
                for half in range(2):
                    store_featmajor_T(lambda ch: sc_pool.t[:, ch, half * 8:(half + 1) * 8, :].rearrange("p s r -> p (s r)"),
                                      sc_pool.d, 120, D, s_pool[l, half * 120:(half + 1) * 120, :])
                store_featmajor_T(lambda ch: sc_conv.t[:, ch, :, :].rearrange("p s r -> p (s r)"), sc_conv.d, 48, CD, s_conv[l])
                store_featmajor_T(lambda ch: sc_ffn.t[:, ch, :, :].rearrange("p s r -> p (s r)"), sc_ffn.d, 32, 2 * DFF, s_ffn[l])
            if last_tile:
                store_featmajor_T(lambda ch: pc_pool[l].t[:, ch, 0, :], pc_pool[l].d, 15, D, p_pool[l])
                store_featmajor_T(lambda ch: pc_conv[l].t[:, ch, 0, :], pc_conv[l].d, 3, CD, p_conv[l])
                store_featmajor_T(lambda ch: pc_ffn[l].t[:, ch, 0, :], pc_ffn[l].d, 2, 2 * DFF, p_ffn[l])

        xnv = R1f[:, 0:8 * T].rearrange("p (c t) -> p c t", c=8)

        def sqf_fin(c):
            b = nsq()
            k.op("ACT", lambda e: e.activation(out=b.t[:, 0:T], in_=x.t[:, c, 0:T], func=AF.Square), R=(x.d,), W=(b.d,))
            return b.t[:, 0:T], b.d
        rms_rstd(sqf_fin, 8, T, float(D), rstd.t[:, 0:T], rstd.d)
        for c in range(8):
            k.op("DVE", lambda e: e.scalar_tensor_tensor(out=xnv[:, c, 0:T], in0=x.t[:, c, 0:T], scalar=fnw.t[:, c:c + 1],
                                                         in1=rstd.t[:, 0:T], op0=ALU.mult, op1=ALU.mult),
                 R=(x.d, rstd.d, PRM), W=R1W)
        if is_ms:
            oblocks = [(NMETA, 128, y_sample)]
        else:
            r0 = tile["idx"] * 512
            oblocks = [(128 * b, 128, y_prompt[r0 + 128 * b:r0 + 128 * (b + 1), :]) for b in range(4)]
        for (c0, n, dst) in oblocks:
            store_featmajor_T(lambda ch: xnv[:, ch, c0:c0 + n], sz_d, n, D, dst)

    k.wait_all("SP", out_dss)
    return nc


_CACHE = {}


def kernel(**inputs):
    if "nc" not in _CACHE:
        _CACHE["nc"] = build_program()
    nc = _CACHE["nc"]
    f = lambda a: np.ascontiguousarray(np.asarray(a, dtype=np.float32))
    shared = {n: f(inputs[n]) for n in ("meta_tokens", "norm1_w", "w_in", "pool_w", "pool_scale", "w_pool_out", "conv_w",
                                         "conv_b", "dt_bias", "a_log", "d_skip", "ssd_norm_w", "w_ssd_out", "w_o",
                                         "norm2_w", "w_up", "ffn_conv_w", "ffn_conv_b", "w_down", "final_norm_w")}
    xp, xs = f(inputs["x_prompt"]), f(inputs["x_sample"])
    sp, sc, sh, sf = f(inputs["state_pool"]), f(inputs["state_conv"]), f(inputs["state_ssm"]), f(inputs["state_ffn"])
    in_maps = []
    for i in range(N_CORES):
        sl = slice(NSEQ * i, NSEQ * (i + 1))
        m = dict(shared)
        m["x_prompt"] = xp[i]
        m["x_sample"] = np.ascontiguousarray(xs[sl].reshape(NSEQ * LS, D))
        m["state_pool"] = np.ascontiguousarray(sp[:, sl].reshape(DEPTH, NSEQ * 15, D))
        m["state_conv"] = np.ascontiguousarray(sc[:, sl].reshape(DEPTH, NSEQ * 3, CD))
        m["state_ssm"] = np.ascontiguousarray(sh[:, sl])
        m["state_ffn"] = np.ascontiguousarray(sf[:, sl].reshape(DEPTH, NSEQ * 2, 2 * DFF))
        in_maps.append(m)
    res = run_bass_kernel_spmd(nc, in_maps, core_ids=list(range(N_CORES)))
    R = res.results
    y_prompt = np.stack([R[i]["y_prompt"] for i in range(N_CORES)], 0)
    y_sample = np.concatenate([R[i]["y_sample"].reshape(NSEQ, LS, D) for i in range(N_CORES)], 0)
    p_pool = np.stack([R[i]["p_pool"] for i in range(N_CORES)], 1)
    p_conv = np.stack([R[i]["p_conv"] for i in range(N_CORES)], 1)
    p_ssm = np.stack([R[i]["p_ssm"] for i in range(N_CORES)], 1)
    p_ffn = np.stack([R[i]["p_ffn"] for i in range(N_CORES)], 1)
    s_pool = np.concatenate([R[i]["s_pool"].reshape(DEPTH, NSEQ, 15, D) for i in range(N_CORES)], 1)
    s_conv = np.concatenate([R[i]["s_conv"].reshape(DEPTH, NSEQ, 3, CD) for i in range(N_CORES)], 1)
    s_ssm = np.concatenate([R[i]["s_ssm"] for i in range(N_CORES)], 1)
    s_ffn = np.concatenate([R[i]["s_ffn"].reshape(DEPTH, NSEQ, 2, 2 * DFF) for i in range(N_CORES)], 1)
    return tuple(np.ascontiguousarray(a.astype(np.float32)) for a in
                 (y_prompt, y_sample, p_pool, p_conv, p_ssm, p_ffn, s_pool, s_conv, s_ssm, s_ffn))
```

```python
import numpy as np
import concourse.bass as bass
import concourse.mybir as mybir
from concourse.bass_utils import run_bass_kernel_spmd

F32 = mybir.dt.float32
BF16 = mybir.dt.bfloat16
AF = mybir.ActivationFunctionType
ALU = mybir.AluOpType
AX = mybir.AxisListType

D = 1024
DEPTH = 4
SEQ = 2048
NMETA = 16
NSEQ = 16
LS = 8
DI = 2048
NH = 32
HP = 64
NG = 4
DS = 128
CD = 3072
DFF = 2816
DIN = 8224
OFF_U, OFF_Z, OFF_XBC, OFF_DT, OFF_GA, OFF_GB = 0, 1024, 3072, 6144, 6176, 7200
EPS = 1e-6
N_CORES = 8
SAME_ENGINE_SYNC = True
import os
STAGE = int(os.environ.get('KSTAGE', '99'))
KLAYERS = int(os.environ.get('KLAYERS', '4'))
KTILES = int(os.environ.get('KTILES', '5'))


class Dep:
    __slots__ = ("name", "w", "r")

    def __init__(self, name=""):
        self.name = name
        self.w = None
        self.r = {}


class DS_:
    def __init__(self, nc, name):
        self.name = name
        self.sem = nc.alloc_semaphore("ds_" + name)
        self.count = 0


class Buf:
    def __init__(self, t, name):
        self.t = t
        self.d = Dep(name)


class KB:
    def __init__(self):
        self.nc = bass.Bass("TRN2", target_bir_lowering=False)
        nc = self.nc
        self.eng = {}
        for name, h in (("PE", nc.tensor), ("ACT", nc.scalar), ("DVE", nc.vector),
                        ("POOL", nc.gpsimd), ("SP", nc.sync)):
            self.eng[name] = dict(h=h, sem=nc.alloc_semaphore("e_" + name), n=0, seen={})
        self.nbuf = 0

    def sb(self, shape, dt, name=None):
        self.nbuf += 1
        name = name or f"b{self.nbuf}"
        return Buf(self.nc.alloc_sbuf_tensor(name, list(shape), dt), name)

    def ps(self, name):
        return Buf(self.nc.alloc_psum_tensor(name, [128, 512], F32), name)

    def _waits(self, ename, reads, writes):
        E = self.eng[ename]
        need = {}

        def add(tok):
            if tok is None:
                return
            key, sem, val = tok
            if key == ename and (ename == "PE" or not SAME_ENGINE_SYNC):
                return
            if key not in need or need[key][2] < val:
                need[key] = tok
        for d in reads:
            add(d.w)
        for d in writes:
            add(d.w)
            for key, t in d.r.items():
                add(t)
        for key, (k2, sem, val) in need.items():
            if E["seen"].get(key, 0) >= val:
                continue
            E["h"].wait_ge(sem, val)
            E["seen"][key] = val

    def op(self, ename, fn, R=(), W=(), inc=True):
        E = self.eng[ename]
        self._waits(ename, R, W)
        ins = fn(E["h"])
        if inc:
            E["n"] += 1
            ins.then_inc(E["sem"], 1)
            pos = E["n"]
        else:
            pos = E["n"] + 1
        tok = (ename, E["sem"], pos)
        for d in R:
            d.r[ename] = tok
        for d in W:
            d.w = tok
            d.r = {}
        return ins

    def dma(self, q, out, in_, ds, R=(), W=(), slow=False):
        E = self.eng[q]
        self._waits(q, R, W)
        if slow:
            ins = E["h"].dma_start(out=out, in_=in_, allow_slow_non_contiguous=True)
        else:
            ins = E["h"].dma_start(out=out, in_=in_)
        ds.count += 16
        ins.then_inc(ds.sem, 16)
        tok = ("dma_" + ds.name, ds.sem, ds.count)
        for d in R:
            d.r[tok[0]] = tok
        for d in W:
            d.w = tok
            d.r = {}

    def wait_all(self, q, dss):
        E = self.eng[q]
        for ds in dss:
            if ds.count:
                E["h"].wait_ge(ds.sem, ds.count)


def build_program():
    k = KB()
    nc = k.nc

    def din(name, shape):
        return nc.dram_tensor(name, list(shape), F32, kind="ExternalInput").ap()

    def dout(name, shape):
        return nc.dram_tensor(name, list(shape), F32, kind="ExternalOutput").ap()

    x_prompt = din("x_prompt", [SEQ, D])
    x_sample = din("x_sample", [NSEQ * LS, D])
    state_pool = din("state_pool", [DEPTH, NSEQ * 15, D])
    state_conv = din("state_conv", [DEPTH, NSEQ * 3, CD])
    state_ssm = din("state_ssm", [DEPTH, NSEQ, NH, HP, DS])
    state_ffn = din("state_ffn", [DEPTH, NSEQ * 2, 2 * DFF])
    meta_tokens = din("meta_tokens", [NMETA, D])
    norm1_w = din("norm1_w", [DEPTH, D])
    w_in = din("w_in", [DEPTH, D, DIN])
    pool_w = din("pool_w", [DEPTH, 4, 256, 256])
    pool_scale = din("pool_scale", [DEPTH, D])
    w_pool_out = din("w_pool_out", [DEPTH, D, D])
    conv_w = din("conv_w", [DEPTH, 4, CD])
    conv_b = din("conv_b", [DEPTH, CD])
    dt_bias = din("dt_bias", [DEPTH, NH])
    a_log = din("a_log", [DEPTH, NH])
    d_skip = din("d_skip", [DEPTH, NH])
    ssd_norm_w = din("ssd_norm_w", [DEPTH, DI])
    w_ssd_out = din("w_ssd_out", [DEPTH, DI, D])
    w_o = din("w_o", [DEPTH, D, D])
    norm2_w = din("norm2_w", [DEPTH, D])
    w_up = din("w_up", [DEPTH, D, 2 * DFF])
    ffn_conv_w = din("ffn_conv_w", [DEPTH, 3, 2 * DFF])
    ffn_conv_b = din("ffn_conv_b", [DEPTH, 2 * DFF])
    w_down = din("w_down", [DEPTH, DFF, D])
    final_norm_w = din("final_norm_w", [D])

    y_prompt = dout("y_prompt", [SEQ, D])
    y_sample = dout("y_sample", [NSEQ * LS, D])
    p_pool = dout("p_pool", [DEPTH, 15, D])
    p_conv = dout("p_conv", [DEPTH, 3, CD])
    p_ssm = dout("p_ssm", [DEPTH, NH, HP, DS])
    p_ffn = dout("p_ffn", [DEPTH, 2, 2 * DFF])
    s_pool = dout("s_pool", [DEPTH, NSEQ * 15, D])
    s_conv = dout("s_conv", [DEPTH, NSEQ * 3, CD])
    s_ssm = dout("s_ssm", [DEPTH, NSEQ, NH, HP, DS])
    s_ffn = dout("s_ffn", [DEPTH, NSEQ * 2, 2 * DFF])
    ssm_scr = nc.dram_tensor("ssm_scr", [DEPTH, NH, HP, DS], F32, kind="Internal").ap()

    out_dss = []

    def new_ds(name, is_out=False):
        ds = DS_(nc, name)
        if is_out:
            out_dss.append(ds)
        return ds

    ident_f = k.sb([128, 128], F32, "ident_f")
    ident_b = k.sb([128, 128], BF16, "ident_b")
    ones_b = k.sb([128, 128], BF16, "ones_b")
    ones_f = k.sb([128, 128], F32, "ones_f")
    triT = k.sb([128, 128], F32, "triT")
    Umat = k.sb([128, 128], F32, "Umat")
    E2 = k.sb([32, 2, 64], F32, "E2")
    Pm = k.sb([32, 16], F32, "Pm")
    tmsk = k.sb([32, 16], F32, "tmsk")
    epsc = k.sb([128, 1], F32, "epsc")
    onec = k.sb([128, 1], F32, "onec")
    icnt = k.sb([128, 4, 16], F32, "icnt")
    CD_ = Dep("consts")

    def pool_op(fn, W=(CD_,), R=()):
        return k.op("POOL", fn, R=R, W=W)

    pool_op(lambda e: e.memset(ones_f.t[:], 1.0))
    pool_op(lambda e: e.memset(ident_f.t[:], 1.0))
    pool_op(lambda e: e.affine_select(out=ident_f.t[:], in_=ident_f.t[:], pattern=[[1, 128]],
                                      compare_op=ALU.is_ge, fill=0.0, base=0, channel_multiplier=-1), R=(CD_,))
    pool_op(lambda e: e.affine_select(out=ident_f.t[:], in_=ident_f.t[:], pattern=[[-1, 128]],
                                      compare_op=ALU.is_ge, fill=0.0, base=0, channel_multiplier=1), R=(CD_,))
    pool_op(lambda e: e.memset(triT.t[:], 1.0))
    pool_op(lambda e: e.affine_select(out=triT.t[:], in_=triT.t[:], pattern=[[1, 128]],
                                      compare_op=ALU.is_ge, fill=0.0, base=0, channel_multiplier=-1), R=(CD_,))
    pool_op(lambda e: e.memset(Umat.t[:], 1.0))
    pool_op(lambda e: e.affine_select(out=Umat.t[:], in_=Umat.t[:], pattern=[[-1, 128]],
                                      compare_op=ALU.is_gt, fill=0.0, base=0, channel_multiplier=1), R=(CD_,))
    par = k.sb([2, 2, 64], F32, "par")
    pool_op(lambda e: e.memset(par.t[:], 1.0))
    pool_op(lambda e: e.affine_select(out=par.t[:], in_=par.t[:], pattern=[[1, 2], [0, 64]],
                                      compare_op=ALU.is_ge, fill=0.0, base=0, channel_multiplier=-1), R=(CD_,))
    pool_op(lambda e: e.affine_select(out=par.t[:], in_=par.t[:], pattern=[[-1, 2], [0, 64]],
                                      compare_op=ALU.is_ge, fill=0.0, base=0, channel_multiplier=1), R=(CD_,))
    e2_scr = nc.dram_tensor("e2_scr", [2, 128], F32, kind="Internal").ap()
    e2_ds = DS_(nc, "e2")
    k.dma("SP", e2_scr, par.t[:].rearrange("r a p -> r (a p)"), e2_ds, R=(CD_,))
    k.eng["SP"]["h"].wait_ge(e2_ds.sem, e2_ds.count)
    for a in range(16):
        k.dma("SP", E2.t[2 * a:2 * a + 2, :, :].rearrange("r a p -> r (a p)"), e2_scr, e2_ds, W=(CD_,))
    pool_op(lambda e: e.memset(Pm.t[:], 1.0))
    pool_op(lambda e: e.affine_select(out=Pm.t[:], in_=Pm.t[:], pattern=[[-2, 16]],
                                      compare_op=ALU.is_ge, fill=0.0, base=0, channel_multiplier=1), R=(CD_,))
    pool_op(lambda e: e.affine_select(out=Pm.t[:], in_=Pm.t[:], pattern=[[2, 16]],
                                      compare_op=ALU.is_ge, fill=0.0, base=1, channel_multiplier=-1), R=(CD_,))
    pool_op(lambda e: e.memset(epsc.t[:], EPS))
    pool_op(lambda e: e.memset(onec.t[:], 1.0))
    for g in range(4):
        pool_op(lambda e: e.iota(icnt.t[:, g, :], [[1, 16]], base=1, channel_multiplier=0,
                                 allow_small_or_imprecise_dtypes=True))
    k.op("DVE", lambda e: e.tensor_copy(out=ident_b.t[:], in_=ident_f.t[:]), R=(CD_,), W=(CD_,))
    k.op("DVE", lambda e: e.tensor_copy(out=ones_b.t[:], in_=ones_f.t[:]), R=(CD_,), W=(CD_,))
    for g in range(4):
        k.op("DVE", lambda e: e.tensor_scalar_min(out=icnt.t[:, g, :], in0=icnt.t[:, g, :],
                                                  scalar1=float(2 ** (g + 1))), R=(CD_,), W=(CD_,))
    k.op("DVE", lambda e: e.reciprocal(out=icnt.t[:], in_=icnt.t[:]), R=(CD_,), W=(CD_,))

    banks = [k.ps(f"pb{i}") for i in range(8)]
    bstate = {"i": 0}

    def nb():
        b = banks[bstate["i"] % 8]
        bstate["i"] += 1
        return b

    TM = 512
    x = k.sb([128, 8, TM], F32, "x")
    hn = k.sb([128, 8, TM], BF16, "hn")
    dbuf = k.sb([128, 8, TM], BF16, "dbuf")
    R1 = k.sb([128, 24 * TM], BF16, "R1")
    R2 = k.sb([128, 16 * TM], BF16, "R2")
    sz_d, bc_d = Dep("sz"), Dep("bc")
    xs_deps = [Dep(f"xs{i}") for i in range(17)]
    scp_d = Dep("scp")
    dtT = k.sb([32, 2, TM], F32, "dtT")
    sg = k.sb([128, 16, TM], BF16, "sg")
    rstd = k.sb([128, TM], F32, "rstd")
    NTMP = 5
    tmpf = [k.sb([128, 544], F32, f"tmpf{i}") for i in range(NTMP)]
    tstate = {"i": 0}

    def ntmp():
        b = tmpf[tstate["i"] % NTMP]
        tstate["i"] += 1
        return b
    NEXT = 3
    extb = [k.sb([128, 544], BF16, f"extb{i}") for i in range(NEXT)]
    exs = {"i": 0}

    def next_ext():
        b = extb[exs["i"] % NEXT]
        exs["i"] += 1
        return b
    NDG = 8
    dgb = [k.sb([128, 128], BF16, f"dg{i}") for i in range(NDG)]
    dgs = {"i": 0}

    def next_dg():
        b = dgb[dgs["i"] % NDG]
        dgs["i"] += 1
        return b
    sqb = [k.sb([128, TM], BF16, f"sqb{i}") for i in range(2)]
    sqs = {"i": 0}

    def nsq():
        b = sqb[sqs["i"] % 2]
        sqs["i"] += 1
        return b
    NSLOT = 3
    wslots = [k.sb([128, 4096], BF16, f"wslot{i}") for i in range(NSLOT)]
    wds = [new_ds(f"w{i}") for i in range(NSLOT)]
    wst = {"i": 0}

    stg = [k.sb([128, 1024], F32, f"stg{i}") for i in range(2)]
    stg_ds = [new_ds(f"stg{i}", is_out=True) for i in range(2)]
    stgs = {"i": 0}

    def nstg():
        i = stgs["i"] % 2
        stgs["i"] += 1
        return stg[i], stg_ds[i]

    prm1 = k.sb([128, DEPTH, 120], F32, "prm1")
    prm2 = k.sb([128, DEPTH, 84], F32, "prm2")
    prm3 = k.sb([128, DEPTH, 132], F32, "prm3")
    fnw = k.sb([128, 8], F32, "fnw")
    dskc = k.sb([128, DEPTH, 16], F32, "dskc")
    hcol = k.sb([32, DEPTH, 4], F32, "hcol")
    a_bc = k.sb([128, DEPTH, 32], F32, "a_bc")
    PRM = Dep("params")
    prm_ds = new_ds("prm")

    def load_rows(dst_stage, r0, src2d, nrows, ds):
        k.dma("SP", dst_stage.t[r0:r0 + nrows, 0:128], src2d, ds, W=(dst_stage.d,))

    for l in range(DEPTH):
        s1, sd1 = nstg()
        load_rows(s1, 0, norm1_w[l].rearrange("(c p) -> c p", p=128), 8, sd1)
        load_rows(s1, 8, norm2_w[l].rearrange("(c p) -> c p", p=128), 8, sd1)
        load_rows(s1, 16, pool_scale[l].rearrange("(c p) -> c p", p=128), 8, sd1)
        load_rows(s1, 24, conv_w[l].rearrange("k (c p) -> (k c) p", p=128), 96, sd1)
        pb = nb()
        k.op("PE", lambda e: e.transpose(pb.t[:, 0:120], s1.t[0:120, 0:128], ident_f.t[0:120, 0:120]),
             R=(s1.d, CD_), W=(pb.d,))
        k.op("ACT", lambda e: e.copy(out=prm1.t[:, l, :], in_=pb.t[:, 0:120]), R=(pb.d,), W=(PRM,))
        s2, sd2 = nstg()
        load_rows(s2, 0, conv_b[l].rearrange("(c p) -> c p", p=128), 24, sd2)
        load_rows(s2, 24, ssd_norm_w[l].rearrange("(c p) -> c p", p=128), 16, sd2)
        load_rows(s2, 40, ffn_conv_b[l].rearrange("(c p) -> c p", p=128), 44, sd2)
        pb = nb()
        k.op("PE", lambda e: e.transpose(pb.t[:, 0:84], s2.t[0:84, 0:128], ident_f.t[0:84, 0:84]),
             R=(s2.d, CD_), W=(pb.d,))
        k.op("ACT", lambda e: e.copy(out=prm2.t[:, l, :], in_=pb.t[:, 0:84]), R=(pb.d,), W=(PRM,))
        s3, sd3 = nstg()
        fw = ffn_conv_w[l].rearrange("k (c p) -> (k c) p", p=128)
        load_rows(s3, 0, fw[0:128], 128, sd3)
        k.dma("SP", s3.t[0:4, 128:256], fw[128:132], sd3, W=(s3.d,))
        pb = nb()
        k.op("PE", lambda e: e.transpose(pb.t[:, 0:128], s3.t[0:128, 0:128], ident_f.t[:, :]),
             R=(s3.d, CD_), W=(pb.d,), inc=False)
        k.op("PE", lambda e: e.transpose(pb.t[:, 128:132], s3.t[0:4, 128:256], ident_f.t[0:4, 0:4]),
             R=(s3.d, CD_), W=(pb.d,))
        k.op("ACT", lambda e: e.copy(out=prm3.t[:, l, :], in_=pb.t[:, 0:132]), R=(pb.d,), W=(PRM,))
        k.dma("SP", hcol.t[:, l, 0:1], dt_bias[l].rearrange("(h o) -> h o", o=1), prm_ds, W=(PRM,), slow=True)
        k.dma("SP", hcol.t[:, l, 1:2], a_log[l].rearrange("(h o) -> h o", o=1), prm_ds, W=(PRM,), slow=True)
        k.dma("SP", hcol.t[:, l, 2:3], d_skip[l].rearrange("(h o) -> h o", o=1), prm_ds, W=(PRM,), slow=True)
        k.dma("SP", a_bc.t[:, l, :], a_log[l].partition_broadcast(128), prm_ds, W=(PRM,))
    sF, sdF = nstg()
    load_rows(sF, 0, final_norm_w.rearrange("(c p) -> c p", p=128), 8, sdF)
    pb = nb()
    k.op("PE", lambda e: e.transpose(pb.t[:, 0:8], sF.t[0:8, 0:128], ident_f.t[0:8, 0:8]),
         R=(sF.d, CD_), W=(pb.d,))
    k.op("ACT", lambda e: e.copy(out=fnw.t[:], in_=pb.t[:, 0:8]), R=(pb.d,), W=(PRM,))
    k.op("ACT", lambda e: e.activation(out=hcol.t[:, :, 1:2], in_=hcol.t[:, :, 1:2], func=AF.Exp), R=(PRM,), W=(PRM,))
    k.op("DVE", lambda e: e.tensor_scalar_mul(out=hcol.t[:, :, 1:2], in0=hcol.t[:, :, 1:2], scalar1=-1.0),
         R=(PRM,), W=(PRM,))
    k.op("ACT", lambda e: e.activation(out=a_bc.t[:], in_=a_bc.t[:], func=AF.Exp), R=(PRM,), W=(PRM,))
    k.op("DVE", lambda e: e.tensor_scalar_mul(out=a_bc.t[:], in0=a_bc.t[:], scalar1=-1.0), R=(PRM,), W=(PRM,))
    Ef = E2.t[:].rearrange("h a p -> h (a p)")
    tmsk_d = Dep("tmsk")

    def expand_heads(col_ap, col_deps, out_fn):
        k.op("DVE", lambda e: e.tensor_scalar_mul(out=tmsk.t[:], in0=Pm.t[:], scalar1=col_ap),
             R=tuple(col_deps) + (CD_,), W=(tmsk_d,))
        pb = nb()
        k.op("PE", lambda e: e.matmul(pb.t[:, 0:16], Ef, tmsk.t[:], start=True, stop=True), R=(tmsk_d, CD_), W=(pb.d,))
        out_fn(pb)

    for l in range(DEPTH):
        expand_heads(hcol.t[:, l, 2:3], (PRM,),
                     lambda pb: k.op("ACT", lambda e: e.copy(out=dskc.t[:, l, :], in_=pb.t[:, 0:16]), R=(pb.d,), W=(PRM,)))

    pc_pool = [k.sb([128, 8, 1, 15], F32, f"pcp{l}") for l in range(DEPTH)]
    pc_conv = [k.sb([128, 24, 1, 3], F32, f"pcc{l}") for l in range(DEPTH)]
    pc_ffn = [k.sb([128, 44, 1, 2], F32, f"pcf{l}") for l in range(DEPTH)]

    class View:
        def __init__(self, t, d):
            self.t, self.d = t, d
    TMS = NMETA + NSEQ * LS
    R1f = R1.t.bitcast(F32)
    R2f = R2.t.bitcast(F32)
    o1 = (24 * TMS) // 2
    sc_conv = View(R1f[:, o1:o1 + 24 * NSEQ * 3].rearrange("p (c s r) -> p c s r", c=24, s=NSEQ), sz_d)
    o2 = o1 + 24 * NSEQ * 3
    sc_ffn = View(R1f[:, o2:o2 + 44 * NSEQ * 2].rearrange("p (c s r) -> p c s r", c=44, s=NSEQ), sz_d)
    o3 = (16 * TMS) // 2
    sc_pool = View(R2f[:, o3:o3 + 8 * NSEQ * 15].rearrange("p (c s r) -> p c s r", c=8, s=NSEQ), scp_d)

    S = k.sb([128, 16, 128], F32, "S")
    Sb = k.sb([128, 16, 128], BF16, "Sb")
    S_ds = new_ds("S", is_out=True)
    S2_ds = new_ds("S2", is_out=True)
    xs_tok = k.sb([128, 32, 64], BF16, "xs_tok")
    B_tok = k.sb([128, 4, 128], BF16, "B_tok")
    xdt = k.sb([128, 32, 64], BF16, "xdt")
    xw = xs_tok
    dtk = k.sb([128, 2, 32], F32, "dtk")
    cumk = k.sb([128, 4, 32], F32, "cumk")
    eL = k.sb([128, 16], F32, "eL")
    totf = k.sb([32, 1], F32, "totf")
    smk = k.sb([128, 4, 128], F32, "smk")
    Rg = [k.sb([128, 8, 128], F32, f"Rg{i}") for i in range(2)]
    dec = Rg
    Mg = [k.sb([128, 8, 128], BF16, f"Mg{i}") for i in range(2)]
    hTg = [k.sb([128, 512], BF16, "hTg0")] * 2
    ytok = [k.sb([128, 512], F32, "ytok0")] * 2
    y_f = [k.sb([128, 16, 128], F32, "y_f0")] * 2
    rstdg = k.sb([128, 4, 128], F32, "rstdg")
    rot = {"i": 0}
    Rg_deps = [Dep(f"Rg_g{i}") for i in range(4)]
    Mg_deps = [Dep(f"Mg_g{i}") for i in range(4)]

    Shn = View(hn.t.bitcast(F32)[:].rearrange("p a b -> p (a b)").rearrange("p (j n) -> p j n", n=128), hn.d)
    print("SBUF bytes remaining per partition:", nc.sbuf_bytes_remaining)

    def wload(srcs):
        i = wst["i"] % NSLOT
        wst["i"] += 1
        slot, ds = wslots[i], wds[i]
        for (src, dstview) in srcs:
            k.dma("POOL", dstview(slot), src, ds, W=(slot.d,))
        return slot

    def rms_rstd(sq_chunks_fn, nchunks, T, dim, out_ap, out_dep, pbank=None, pslice=None):
        pb = pbank or nb()
        pv = pslice if pslice is not None else pb.t[:, 0:T]
        for c in range(nchunks):
            sq, sqd = sq_chunks_fn(c)
            k.op("PE", lambda e: e.matmul(pv, ones_b.t[:, :], sq, start=(c == 0), stop=(c == nchunks - 1)),
                 R=(sqd, CD_), W=(pb.d,), inc=True)
        k.op("ACT", lambda e: e.activation(out=out_ap, in_=pv, func=AF.Sqrt, bias=epsc.t[:, 0:1], scale=1.0 / dim),
             R=(pb.d, CD_), W=(out_dep,))
        k.op("DVE", lambda e: e.reciprocal(out=out_ap, in_=out_ap), R=(out_dep,), W=(out_dep,))

    def rmsnorm_to(dst, T, wcol_fn):
        def sqf(c):
            b = nsq()
            k.op("ACT", lambda e: e.activation(out=b.t[:, 0:T], in_=x.t[:, c, 0:T], func=AF.Square),
                 R=(x.d,), W=(b.d,))
            return b.t[:, 0:T], b.d
        rms_rstd(sqf, 8, T, float(D), rstd.t[:, 0:T], rstd.d)
        for c in range(8):
            k.op("DVE", lambda e: e.scalar_tensor_tensor(out=dst.t[:, c, 0:T], in0=x.t[:, c, 0:T], scalar=wcol_fn(c),
                                                         in1=rstd.t[:, 0:T], op0=ALU.mult, op1=ALU.mult),
                 R=(x.d, rstd.d, PRM), W=(dst.d,))

    def load_tokmajor_T(src2d, nrows, ncols, dst_fn, dst_dep):
        for c0 in range(0, ncols, 1024):
            w = min(1024, ncols - c0)
            st, sds = nstg()
            k.dma("SP", st.t[0:nrows, 0:w], src2d[:, c0:c0 + w], sds, W=(st.d,))
            for q0 in range(0, w, 512):
                qn = min(512, w - q0)
                pb = nb()
                nchk = qn // 128
                for cc in range(nchk):
                    k.op("PE", lambda e: e.transpose(pb.t[:, cc * 128:cc * 128 + nrows],
                                                     st.t[0:nrows, q0 + cc * 128:q0 + (cc + 1) * 128],
                                                     ident_f.t[0:nrows, 0:nrows]),
                         R=(st.d, CD_), W=(pb.d,), inc=(cc == nchk - 1))
                for cc in range(nchk):
                    ch = (c0 + q0) // 128 + cc
                    k.op("ACT", lambda e: e.copy(out=dst_fn(ch), in_=pb.t[:, cc * 128:cc * 128 + nrows]),
                         R=(pb.d,), W=(dst_dep,))

    def store_featmajor_T(src_fn, src_dep, nrows, ncols, dst2d, out_ds_unused=None):
        for c0 in range(0, ncols, 1024):
            w = min(1024, ncols - c0)
            st, sds = nstg()
            for q0 in range(0, w, 512):
                qn = min(512, w - q0)
                pb = nb()
                nchk = qn // 128
                for cc in range(nchk):
                    ch = (c0 + q0) // 128 + cc
                    k.op("PE", lambda e: e.transpose(pb.t[0:nrows, cc * 128:(cc + 1) * 128], src_fn(ch),
                                                     ident_f.t[:, :]),
                         R=(src_dep, CD_), W=(pb.d,), inc=(cc == nchk - 1))
                k.op("ACT", lambda e: e.copy(out=st.t[0:nrows, q0:q0 + qn], in_=pb.t[0:nrows, 0:qn]),
                     R=(pb.d,), W=(st.d,))
            k.dma("SP", dst2d[:, c0:c0 + w], st.t[0:nrows, 0:w], sds, R=(st.d,))

    class Seg:
        def __init__(self, kind, c0, nseq, L):
            self.kind, self.c0, self.nseq, self.L = kind, c0, nseq, L

    tiles = [dict(kind="MS", T=NMETA + NSEQ * LS, segs=[Seg("meta", 0, 1, NMETA), Seg("samp", NMETA, NSEQ, LS)],
                  chunks=[("meta", 0, NMETA, None)] + [("samp", NMETA + LS * j, LS, j) for j in range(NSEQ)])]
    for i in range(4):
        tiles.append(dict(kind="P", idx=i, T=512, segs=[Seg("prompt", 0, 1, 512)],
                          chunks=[("prompt", 128 * c, 128, None) for c in range(4)]))

    def carry_of(seg, l, which):
        if seg.kind == "samp":
            return {"pool": sc_pool, "conv": sc_conv, "ffn": sc_ffn}[which]
        return {"pool": pc_pool, "conv": pc_conv, "ffn": pc_ffn}[which][l]

    def evac_to_ext(pb, T, seg, PRE, carry, ch):
        eb = ntmp()
        n, L = seg.nseq, seg.L
        ev = eb.t[:, 0:n * (PRE + L)].rearrange("p (s l) -> p s l", l=PRE + L)
        k.op("ACT", lambda e: e.copy(out=ev[:, :, PRE:PRE + L],
                                     in_=pb.t[:, seg.c0:seg.c0 + n * L].rearrange("p (s l) -> p s l", l=L)),
             R=(pb.d,), W=(eb.d,))
        k.op("DVE", lambda e: e.tensor_copy(out=ev[:, :, 0:PRE], in_=carry.t[:, ch, :, :]),
             R=(carry.d,), W=(eb.d,))
        k.op("DVE", lambda e: e.tensor_copy(out=carry.t[:, ch, :, :], in_=ev[:, :, L:L + PRE]),
             R=(eb.d,), W=(carry.d,))
        return eb, ev

    def conv_taps(eb, ev, seg, ntaps, wcols, bcol):
        n, L = seg.nseq, seg.L
        acc = ntmp()
        av = acc.t[:, 0:n * L].rearrange("p (s l) -> p s l", l=L)
        k.op("DVE", lambda e: e.tensor_scalar(out=av, in0=ev[:, :, 0:L], scalar1=wcols[0], scalar2=bcol,
                                              op0=ALU.mult, op1=ALU.add), R=(eb.d, PRM), W=(acc.d,))
        for t in range(1, ntaps):
            k.op("DVE", lambda e: e.scalar_tensor_tensor(out=av, in0=ev[:, :, t:t + L], scalar=wcols[t], in1=av,
                                                         op0=ALU.mult, op1=ALU.add), R=(eb.d, acc.d, PRM), W=(acc.d,))
        return acc, av

    def conv_pe(pb, seg, PRE, carry, ch, wcols):
        n, L = seg.nseq, seg.L
        eb = next_ext()
        ev = eb.t[:, 0:n * (PRE + L)].rearrange("p (s l) -> p s l", l=PRE + L)
        pv = pb.t[:, seg.c0:seg.c0 + n * L].rearrange("p (s l) -> p s l", l=L)
        k.op("ACT", lambda e: e.copy(out=ev[:, :, PRE:PRE + L], in_=pv), R=(pb.d,), W=(eb.d,))
        k.op("DVE", lambda e: e.tensor_copy(out=ev[:, :, 0:PRE], in_=carry.t[:, ch, :, :]), R=(carry.d,), W=(eb.d,))
        k.op("DVE", lambda e: e.tensor_copy(out=carry.t[:, ch, :, :], in_=pv[:, :, L - PRE:L]), R=(pb.d,), W=(carry.d,))
        dgs_ = []
        for wc in wcols:
            dg = next_dg()
            k.op("ACT", lambda e: e.activation(out=dg.t[:, :], in_=ident_f.t[:, :], func=AF.Copy, scale=wc),
                 R=(CD_, PRM), W=(dg.d,))
            dgs_.append(dg)
        pc = nb()
        nt = len(wcols)
        if n == 1 or os.environ.get("K3D", "0") == "1":
            pcv3 = pc.t[:, 0:n * L].rearrange("p (s l) -> p s l", l=L)
            for t_ in range(nt):
                k.op("PE", lambda e: e.matmul(pcv3 if n > 1 else pc.t[:, 0:L], dgs_[t_].t[:, :],
                                              ev[:, :, t_:t_ + L] if n > 1 else ev[:, 0, t_:t_ + L],
                                              start=(t_ == 0), stop=(t_ == nt - 1)),
                     R=(dgs_[t_].d, eb.d), W=(pc.d,), inc=True)
        else:
            for s_ in range(n):
                for t_ in range(nt):
                    k.op("PE", lambda e: e.matmul(pc.t[:, s_ * L:(s_ + 1) * L], dgs_[t_].t[:, :], ev[:, s_, t_:t_ + L],
                                                  start=(t_ == 0), stop=(t_ == nt - 1)),
                         R=(dgs_[t_].d, eb.d), W=(pc.d,), inc=(t_ == nt - 1))
        return pc, pc.t[:, 0:n * L].rearrange("p (s l) -> p s l", l=L)

    for ti, tile in enumerate(tiles[:KTILES]):
        T = tile["T"]
        is_ms = tile["kind"] == "MS"
        last_tile = (ti == len(tiles) - 1)
        nchunks_ssd = len(tile["chunks"])
        szv = R1.t[:, 0:16 * T].rearrange("p (c t) -> p c t", t=T)
        bcv = R1.t[:, 16 * T:24 * T].rearrange("p (c t) -> p c t", t=T)
        gv = R1.t[:, 0:22 * T].rearrange("p (c t) -> p c t", t=T)
        xsv = R2.t[:, 0:16 * T].rearrange("p (c t) -> p c t", t=T)
        R1W = (sz_d, bc_d)

        if is_ms:
            blocks = [(0, NMETA, meta_tokens), (NMETA, 128, x_sample)]
        else:
            r0 = tile["idx"] * 512
            blocks = [(128 * b, 128, x_prompt[r0 + 128 * b:r0 + 128 * (b + 1), :]) for b in range(4)]
        for (c0, n, src) in blocks:
            load_tokmajor_T(src, n, D, lambda ch: x.t[:, ch, c0:c0 + n], x.d)

        for l in range(KLAYERS if STAGE >= 1 else 0):
            n1c = lambda c: prm1.t[:, l, c:c + 1]
            n2c = lambda c: prm1.t[:, l, 8 + c:9 + c]
            psc = lambda c: prm1.t[:, l, 16 + c:17 + c]
            cwc = lambda t_, c: prm1.t[:, l, 24 + t_ * 24 + c:25 + t_ * 24 + c]
            cbc = lambda c: prm2.t[:, l, c:c + 1]
            snc = lambda c: prm2.t[:, l, 24 + c:25 + c]
            fbc = lambda c: prm2.t[:, l, 40 + c:41 + c]
            fwc = lambda t_, c: prm3.t[:, l, t_ * 44 + c:t_ * 44 + c + 1]

            if is_ms:
                k.op("DVE", lambda e: e.memset(pc_pool[l].t[:], 0.0), W=(pc_pool[l].d,))
                k.op("DVE", lambda e: e.memset(pc_conv[l].t[:], 0.0), W=(pc_conv[l].d,))
                k.op("DVE", lambda e: e.memset(pc_ffn[l].t[:], 0.0), W=(pc_ffn[l].d,))
                for half in range(2):
                    load_tokmajor_T(state_pool[l, half * 120:(half + 1) * 120, :], 120, D,
                                    lambda ch: sc_pool.t[:, ch, half * 8:(half + 1) * 8, :].rearrange("p s r -> p (s r)"),
                                    sc_pool.d)
                load_tokmajor_T(state_conv[l], 48, CD,
                                lambda ch: sc_conv.t[:, ch, :, :].rearrange("p s r -> p (s r)"), sc_conv.d)
                load_tokmajor_T(state_ffn[l], 32, 2 * DFF,
                                lambda ch: sc_ffn.t[:, ch, :, :].rearrange("p s r -> p (s r)"), sc_ffn.d)

            rmsnorm_to(hn, T, n1c)

            def inproj_block(col0, ncols, evac):
                nblk = ncols
                slot = wload([(w_in[l, :, col0:col0 + ncols].rearrange("(kt p) n -> p kt n", p=128),
                               lambda s: s.t[:, 0:8 * ncols].rearrange("p (kt n) -> p kt n", n=ncols))])
                wv = slot.t[:, 0:8 * ncols].rearrange("p (kt n) -> p kt n", n=ncols)
                for o in range(0, ncols, 128):
                    m = min(128, ncols - o)
                    pb = nb()
                    for kt in range(8):
                        k.op("PE", lambda e: e.matmul(pb.t[0:m, 0:T], wv[:, kt, o:o + m], hn.t[:, kt, 0:T],
                                                      start=(kt == 0), stop=(kt == 7)),
                             R=(slot.d, hn.d), W=(pb.d,), inc=(kt == 7))
                    evac(pb, (col0 + o))

            def evac_u(pb, col):
                ch = (col - OFF_U) // 128
                g = ch // 2
                win = 2 ** (g + 1)
                for seg in tile["segs"]:
                    carry = carry_of(seg, l, "pool")
                    eb, ev = evac_to_ext(pb, T, seg, 15, carry, ch)
                    n, L = seg.nseq, seg.L
                    W_ = 15 + L
                    cur, curv = eb, ev
                    sh = 1
                    lo = 0
                    while sh < win:
                        lo += sh
                        nbuf_ = ntmp()
                        nv = nbuf_.t[:, 0:n * W_].rearrange("p (s l) -> p s l", l=W_)
                        cv = curv
                        k.op("DVE", lambda e: e.tensor_tensor(out=nv[:, :, lo:W_], in0=cv[:, :, lo:W_],
                                                              in1=cv[:, :, lo - sh:W_ - sh], op=ALU.add),
                             R=(cur.d,), W=(nbuf_.d,))
                        cur, curv = nbuf_, nv
                        sh *= 2
                    dv = dbuf.t[:, ch, seg.c0:seg.c0 + n * L].rearrange("p (s l) -> p s l", l=L)
                    if seg.kind == "meta":
                        k.op("DVE", lambda e: e.tensor_tensor(out=curv[:, :, 15:15 + L], in0=curv[:, :, 15:15 + L],
                                                              in1=icnt.t[:, g:g + 1, 0:L], op=ALU.mult),
                             R=(cur.d, CD_), W=(cur.d,))
                        k.op("DVE", lambda e: e.tensor_tensor(out=dv, in0=curv[:, :, 15:15 + L],
                                                              in1=ev[:, :, 15:15 + L], op=ALU.subtract),
                             R=(cur.d, eb.d), W=(dbuf.d,))
                    else:
                        k.op("DVE", lambda e: e.scalar_tensor_tensor(out=dv, in0=curv[:, :, 15:15 + L],
                                                                     scalar=1.0 / win, in1=ev[:, :, 15:15 + L],
                                                                     op0=ALU.mult, op1=ALU.subtract),
                             R=(cur.d, eb.d), W=(dbuf.d,))

            def evac_z(pb, col):
                ch = (col - OFF_Z) // 128
                k.op("ACT", lambda e: e.activation(out=szv[:, ch, 0:T], in_=pb.t[:, 0:T], func=AF.Silu),
                     R=(pb.d,), W=R1W)

            def evac_xbc(pb, col):
                ch = (col - OFF_XBC) // 128
                for seg in tile["segs"]:
                    carry = carry_of(seg, l, "conv")
                    pc, pcv = conv_pe(pb, seg, 3, carry, ch, [cwc(t_, ch) for t_ in range(4)])
                    n, L = seg.nseq, seg.L
                    if ch < 16:
                        dst, dd = xsv[:, ch, seg.c0:seg.c0 + n * L], tuple(xs_deps) + (scp_d,)
                    else:
                        dst, dd = bcv[:, ch - 16, seg.c0:seg.c0 + n * L], R1W
                    k.op("ACT", lambda e: e.activation(out=dst.rearrange("p (s l) -> p s l", l=L), in_=pcv, func=AF.Silu,
                                                       bias=cbc(ch), scale=1.0), R=(pc.d, PRM), W=tuple(dd))

            def evac_dt(pb, col):
                k.op("ACT", lambda e: e.activation(out=dtT.t[:, 0, 0:T], in_=pb.t[0:32, 0:T], func=AF.Exp,
                                                   bias=hcol.t[:, l, 0:1], scale=1.0), R=(pb.d, PRM), W=(dtT.d,))
                k.op("ACT", lambda e: e.activation(out=dtT.t[:, 0, 0:T], in_=dtT.t[:, 0, 0:T], func=AF.Ln,
                                                   bias=onec.t[0:32, 0:1], scale=1.0), R=(dtT.d, CD_), W=(dtT.d,))
                k.op("DVE", lambda e: e.tensor_scalar_mul(out=dtT.t[:, 1, 0:T], in0=dtT.t[:, 0, 0:T],
                                                          scalar1=hcol.t[:, l, 1:2]), R=(dtT.d, PRM), W=(dtT.d,))

            def evac_g(pb, col):
                ch = (col - OFF_GA) // 128
                k.op("ACT", lambda e: e.activation(out=sg.t[:, ch, 0:T], in_=pb.t[:, 0:T], func=AF.Sigmoid),
                     R=(pb.d,), W=(sg.d,))

            for b in range(2):
                inproj_block(OFF_U + 512 * b, 512, evac_u)
            for b in range(4):
                inproj_block(OFF_Z + 512 * b, 512, evac_z)
            for b in range(6):
                inproj_block(OFF_XBC + 512 * b, 512, evac_xbc)
            inproj_block(OFF_DT, 32, evac_dt)
            for b in range(4):
                inproj_block(OFF_GA + 512 * b, 512, evac_g)

            if STAGE < 2:
                continue
            slot = wload([(pool_w[l].rearrange("g (kt p) n -> p g kt n", p=128),
                           lambda s: s.t[:, 0:2048].rearrange("p (g kt n) -> p g kt n", g=4, kt=2))])
            pwv = slot.t[:, 0:2048].rearrange("p (g kt n) -> p g kt n", g=4, kt=2)
            mixed = hn
            for g in range(4):
                for o in range(2):
                    pb = nb()
                    for kt in range(2):
                        k.op("PE", lambda e: e.matmul(pb.t[:, 0:T], pwv[:, g, kt, o * 128:(o + 1) * 128],
                                                      dbuf.t[:, 2 * g + kt, 0:T], start=(kt == 0), stop=(kt == 1)),
                             R=(slot.d, dbuf.d), W=(pb.d,), inc=(kt == 1))
                    ch = 2 * g + o
                    k.op("ACT", lambda e: e.activation(out=mixed.t[:, ch, 0:T], in_=pb.t[:, 0:T], func=AF.Copy,
                                                       scale=psc(ch)), R=(pb.d, PRM), W=(mixed.d,))
            for b in range(2):
                slot = wload([(w_pool_out[l, :, 512 * b:512 * (b + 1)].rearrange("(kt p) n -> p kt n", p=128),
                               lambda s: s.t[:, 0:4096].rearrange("p (kt n) -> p kt n", n=512))])
                wv = slot.t[:, 0:4096].rearrange("p (kt n) -> p kt n", n=512)
                for o in range(4):
                    pb = nb()
                    for kt in range(8):
                        k.op("PE", lambda e: e.matmul(pb.t[:, 0:T], wv[:, kt, o * 128:(o + 1) * 128],
                                                      mixed.t[:, kt, 0:T], start=(kt == 0), stop=(kt == 7)),
                             R=(slot.d, mixed.d), W=(pb.d,), inc=(kt == 7))
                    ch = 4 * b + o
                    k.op("DVE", lambda e: e.tensor_tensor(out=dbuf.t[:, ch, 0:T], in0=pb.t[:, 0:T],
                                                          in1=sg.t[:, ch, 0:T], op=ALU.mult),
                         R=(pb.d, sg.d), W=(dbuf.d,))

            for ci, (ckind, c0, Q, sj) in enumerate(tile["chunks"] if STAGE >= 3 else []):
                Sc, Sc_ds = S, S_ds
                if ckind == "meta":
                    k.op("DVE", lambda e: e.memset(S.t[:], 0.0), W=(S.d,))
                    k.op("DVE", lambda e: e.memset(Sb.t[:], 0.0), W=(Sb.d,))
                elif ckind == "samp":
                    sbufs = [(Shn, S2_ds), (S, S_ds)]

                    def load_samp(j):
                        b_, ds_ = sbufs[j % 2]
                        sv_ = state_ssm[l, j].rearrange("(j two) p n -> two p j n", two=2)
                        for two in range(2):
                            k.dma("SP", b_.t[two * 64:(two + 1) * 64, :, :], sv_[two], ds_, W=(b_.d,))
                    if sj == 0:
                        load_samp(0)
                    if sj + 1 < NSEQ:
                        load_samp(sj + 1)
                    Sc, Sc_ds = sbufs[sj % 2]
                    k.op("ACT", lambda e: e.copy(out=Sb.t[:], in_=Sc.t[:]), R=(Sc.d,), W=(Sb.d,))
                elif ckind == "prompt" and ci == 0:
                    sv = ssm_scr[l].rearrange("(j two) p n -> two p j n", two=2)
                    for two in range(2):
                        k.dma("SP", S.t[two * 64:(two + 1) * 64, :, :], sv[two], S_ds, W=(S.d,))
                    k.op("ACT", lambda e: e.copy(out=Sb.t[:], in_=S.t[:]), R=(S.d,), W=(Sb.d,))
                xd = xs_deps[ci]
                for r in range(2):
                    pb = nb()
                    pbv = pb.t.bitcast(BF16)
                    for jj in range(8):
                        j = 8 * r + jj
                        k.op("PE", lambda e: e.transpose(pbv[0:Q, jj * 128:(jj + 1) * 128], xsv[:, j, c0:c0 + Q],
                                                         ident_b.t[:, :]), R=(xd, CD_), W=(pb.d,), inc=(jj == 7))
                    k.op("ACT", lambda e: e.copy(out=xs_tok.t[0:Q, 16 * r:16 * (r + 1), :].rearrange("q h p -> q (h p)"),
                                                 in_=pbv[0:Q, 0:1024]), R=(pb.d,), W=(xs_tok.d,))
                pb = nb()
                pbv = pb.t.bitcast(BF16)
                for g in range(4):
                    k.op("PE", lambda e: e.transpose(pbv[0:Q, g * 128:(g + 1) * 128], bcv[:, g, c0:c0 + Q],
                                                     ident_b.t[:, :]), R=(bc_d, CD_), W=(pb.d,), inc=(g == 3))
                k.op("ACT", lambda e: e.copy(out=B_tok.t[0:Q, :, :].rearrange("q g n -> q (g n)"), in_=pbv[0:Q, 0:512]),
                     R=(pb.d,), W=(B_tok.d,))
                pb = nb()
                for w_ in range(2):
                    k.op("PE", lambda e: e.transpose(pb.t[0:Q, 32 * w_:32 * (w_ + 1)], dtT.t[:, w_, c0:c0 + Q],
                                                     ident_f.t[0:32, 0:32]), R=(dtT.d, CD_), W=(pb.d,), inc=(w_ == 1))
                k.op("ACT", lambda e: e.copy(out=dtk.t[0:Q, :, :].rearrange("q a h -> q (a h)"), in_=pb.t[0:Q, 0:64]),
                     R=(pb.d,), W=(dtk.d,))
                pb = nb()
                k.op("PE", lambda e: e.matmul(pb.t[0:Q, 0:32], triT.t[0:Q, 0:Q], dtk.t[0:Q, 1, :], start=True, stop=True),
                     R=(dtk.d, CD_), W=(pb.d,), inc=False)
                k.op("PE", lambda e: e.matmul(pb.t[0:Q, 32:64], ones_f.t[0:Q, 0:Q], dtk.t[0:Q, 1, :], start=True, stop=True),
                     R=(dtk.d, CD_), W=(pb.d,))
                k.op("ACT", lambda e: e.copy(out=cumk.t[0:Q, 0, :], in_=pb.t[0:Q, 0:32]), R=(pb.d,), W=(cumk.d,))
                k.op("ACT", lambda e: e.activation(out=cumk.t[0:Q, 1, :], in_=pb.t[0:Q, 0:32], func=AF.Exp),
                     R=(pb.d,), W=(cumk.d,))
                k.op("DVE", lambda e: e.tensor_tensor(out=cumk.t[0:Q, 2, :], in0=pb.t[0:Q, 32:64], in1=cumk.t[0:Q, 0, :],
                                                      op=ALU.subtract), R=(pb.d, cumk.d), W=(cumk.d,))
                k.op("ACT", lambda e: e.activation(out=cumk.t[0:Q, 2, :], in_=cumk.t[0:Q, 2, :], func=AF.Exp),
                     R=(cumk.d,), W=(cumk.d,))
                k.op("DVE", lambda e: e.reduce_sum(out=totf.t[:, 0:1], in_=dtT.t[:, 1, c0:c0 + Q], axis=AX.X),
                     R=(dtT.d,), W=(totf.d,))
                expand_heads(totf.t[:, 0:1], (totf.d,),
                             lambda pb: k.op("ACT", lambda e: e.activation(out=eL.t[:, :], in_=pb.t[:, 0:16], func=AF.Exp),
                                             R=(pb.d,), W=(eL.d,)))
                pb = nb()
                for g in range(4):
                    k.op("PE", lambda e: e.matmul(pb.t[0:Q, g * 128:g * 128 + Q], bcv[:, g, c0:c0 + Q],
                                                  bcv[:, 4 + g, c0:c0 + Q], start=True, stop=True),
                         R=(bc_d,), W=(pb.d,), inc=(g == 3))
                k.op("DVE", lambda e: e.tensor_tensor(out=smk.t[0:Q, :, 0:Q],
                                                      in0=pb.t[0:Q, :].rearrange("q (g t) -> q g t", g=4)[:, :, 0:Q],
                                                      in1=triT.t[0:Q, 0:Q].unsqueeze(1).broadcast_to([Q, 4, Q]),
                                                      op=ALU.mult), R=(pb.d, CD_), W=(smk.d,))
                k.op("DVE", lambda e: e.tensor_tensor(out=xdt.t[0:Q, :, :], in0=xs_tok.t[0:Q, :, :],
                                                      in1=dtk.t[0:Q, 0, :].unsqueeze(2).broadcast_to([Q, 32, 64]),
                                                      op=ALU.mult), R=(xs_tok.d, dtk.d), W=(xdt.d,))
                k.op("DVE", lambda e: e.tensor_tensor(out=xw.t[0:Q, :, :], in0=xdt.t[0:Q, :, :],
                                                      in1=cumk.t[0:Q, 2, :].unsqueeze(2).broadcast_to([Q, 32, 64]),
                                                      op=ALU.mult), R=(xdt.d, cumk.d), W=(xw.d,))
                yf = y_f[rot["i"] % 2]
                rot["i"] += 1
                def front(g):
                    R_, dc, M_, hT, yt = Rg[g % 2], dec[g % 2], Mg[g % 2], hTg[g % 2], ytok[g % 2]
                    if Q <= 32:
                        go, Rd, Md = g * Q, Rg_deps[g], Mg_deps[g]
                    else:
                        go, Rd, Md = 0, R_.d, M_.d
                    k.op("DVE", lambda e: e.tensor_tensor(out=R_.t[0:Q, :, go:go + Q],
                                                          in0=dtk.t[0:Q, 1, 8 * g:8 * g + 8].unsqueeze(2).broadcast_to([Q, 8, Q]),
                                                          in1=triT.t[0:Q, 0:Q].unsqueeze(1).broadcast_to([Q, 8, Q]),
                                                          op=ALU.mult), R=(dtk.d, CD_), W=(Rd,))
                    hp = 4 if Q == 128 else 8
                    for hb in range(0, 8, hp):
                        pb = nb()
                        if Q == 128:
                            k.op("PE", lambda e: e.matmul(pb.t[0:Q, 0:512], Umat.t[0:Q, 0:Q],
                                                          R_.t[0:Q, hb:hb + 4, :].rearrange("q h t -> q (h t)"),
                                                          start=True, stop=True), R=(Rd, CD_), W=(pb.d,))
                            pv = pb.t[0:Q, 0:512].rearrange("q (h t) -> q h t", h=4)
                        else:
                            for hh in range(8):
                                k.op("PE", lambda e: e.matmul(pb.t[0:Q, hh * Q:(hh + 1) * Q], Umat.t[0:Q, 0:Q],
                                                              R_.t[0:Q, hh, go:go + Q], start=True, stop=True),
                                     R=(Rd, CD_), W=(pb.d,), inc=(hh == 7))
                            pv = pb.t[0:Q, 0:8 * Q].rearrange("q (h t) -> q h t", h=8)
                        k.op("ACT", lambda e: e.activation(out=dc.t[0:Q, hb:hb + hp, go:go + Q], in_=pv, func=AF.Exp),
                             R=(pb.d,), W=(Rd,))
                    k.op("DVE", lambda e: e.tensor_tensor(out=M_.t[0:Q, :, go:go + Q], in0=dc.t[0:Q, :, go:go + Q],
                                                          in1=smk.t[0:Q, g, 0:Q].unsqueeze(1).broadcast_to([Q, 8, Q]),
                                                          op=ALU.mult), R=(Rd, smk.d), W=(Md,))
                def back(g):
                    R_, dc, M_, hT, yt = Rg[g % 2], dec[g % 2], Mg[g % 2], hTg[g % 2], ytok[g % 2]
                    if Q <= 32:
                        go, Rd, Md = g * Q, Rg_deps[g], Mg_deps[g]
                    else:
                        go, Rd, Md = 0, R_.d, M_.d
                    pbY = nb()
                    for hh in range(8):
                        h = 8 * g + hh
                        k.op("PE", lambda e: e.matmul(pbY.t[0:Q, hh * 64:(hh + 1) * 64], M_.t[0:Q, hh, go:go + Q],
                                                      xdt.t[0:Q, h, :], start=True, stop=True),
                             R=(Md, xdt.d), W=(pbY.d,), inc=(hh == 7))
                    pbT = nb()
                    pbTv = pbT.t.bitcast(BF16)
                    for jj in range(4):
                        j = 4 * g + jj
                        k.op("PE", lambda e: e.transpose(pbTv[:, jj * 128:(jj + 1) * 128], Sb.t[:, j, :], ident_b.t[:, :]),
                             R=(Sb.d, CD_), W=(pbT.d,), inc=(jj == 3))
                    k.op("ACT", lambda e: e.copy(out=hT.t[:, :], in_=pbTv[:, 0:512]), R=(pbT.d,), W=(hT.d,))
                    pbC = nb()
                    k.op("PE", lambda e: e.matmul(pbC.t[0:Q, 0:512], bcv[:, 4 + g, c0:c0 + Q], hT.t[:, :],
                                                  start=True, stop=True), R=(bc_d, hT.d), W=(pbC.d,))
                    k.op("DVE", lambda e: e.tensor_tensor(out=yt.t[0:Q, :].rearrange("q (h p) -> q h p", h=8),
                                                          in0=pbC.t[0:Q, :].rearrange("q (h p) -> q h p", h=8),
                                                          in1=cumk.t[0:Q, 1, 8 * g:8 * g + 8].unsqueeze(2).broadcast_to([Q, 8, 64]),
                                                          op=ALU.mult), R=(pbC.d, cumk.d), W=(yt.d,))
                    k.op("DVE", lambda e: e.tensor_tensor(out=yt.t[0:Q, :], in0=yt.t[0:Q, :], in1=pbY.t[0:Q, 0:512],
                                                          op=ALU.add), R=(pbY.d, yt.d), W=(yt.d,))
                    pbF = nb()
                    for jj in range(4):
                        k.op("PE", lambda e: e.transpose(pbF.t[:, jj * 128:jj * 128 + Q], yt.t[0:Q, jj * 128:(jj + 1) * 128],
                                                         ident_f.t[0:Q, 0:Q]), R=(yt.d, CD_), W=(pbF.d,), inc=(jj == 3))
                    k.op("ACT", lambda e: e.copy(out=yf.t[:, 4 * g:4 * g + 4, 0:Q],
                                                 in_=pbF.t[:, :].rearrange("p (j t) -> p j t", j=4)[:, :, 0:Q]),
                         R=(pbF.d,), W=(yf.d,))
                    pbD = nb()
                    for jj in range(4):
                        j = 4 * g + jj
                        k.op("PE", lambda e: e.matmul(pbD.t[:, jj * 128:(jj + 1) * 128],
                                                      xw.t[0:Q, 2 * j:2 * j + 2, :].rearrange("q h p -> q (h p)"),
                                                      B_tok.t[0:Q, g, :], start=True, stop=True),
                             R=(xw.d, B_tok.d), W=(pbD.d,), inc=(jj == 3))
                    Sg = Sc.t[:, 4 * g:4 * g + 4, :]
                    k.op("DVE", lambda e: e.tensor_tensor(out=Sg, in0=Sg,
                                                          in1=eL.t[:, 4 * g:4 * g + 4].unsqueeze(2).broadcast_to([128, 4, 128]),
                                                          op=ALU.mult), R=(Sc.d, eL.d), W=(Sc.d,))
                    k.op("DVE", lambda e: e.tensor_tensor(out=Sg, in0=Sg,
                                                          in1=pbD.t[:, :].rearrange("p (j n) -> p j n", j=4),
                                                          op=ALU.add), R=(Sc.d, pbD.d), W=(Sc.d,))
                    if ckind != "samp":
                        k.op("ACT", lambda e: e.copy(out=Sb.t[:, 4 * g:4 * g + 4, :], in_=Sg), R=(Sc.d,), W=(Sb.d,))
                front(0)
                for g in range(4):
                    if g + 1 < 4:
                        front(g + 1)
                    back(g)
                if ckind == "samp":
                    dv = s_ssm[l, sj].rearrange("(j two) p n -> two p j n", two=2)
                    for two in range(2):
                        k.dma("SP", dv[two], Sc.t[two * 64:(two + 1) * 64, :, :], Sc_ds, R=(Sc.d,))
                elif ckind == "meta" or ci == nchunks_ssd - 1:
                    dst = p_ssm[l] if (last_tile and ckind == "prompt") else ssm_scr[l]
                    dv = dst.rearrange("(j two) p n -> two p j n", two=2)
                    for two in range(2):
                        k.dma("SP", dv[two], S.t[two * 64:(two + 1) * 64, :, :], S_ds, R=(S.d,))
                k.op("DVE", lambda e: e.tensor_tensor(out=xsv[:, :, c0:c0 + Q], in0=xsv[:, :, c0:c0 + Q],
                                                      in1=dskc.t[:, l, :].unsqueeze(2).broadcast_to([128, 16, Q]),
                                                      op=ALU.mult), R=(xd, PRM), W=(xd,))
                k.op("DVE", lambda e: e.tensor_tensor(out=yf.t[:, :, 0:Q], in0=yf.t[:, :, 0:Q], in1=xsv[:, :, c0:c0 + Q],
                                                      op=ALU.add), R=(xd, yf.d), W=(yf.d,))
                k.op("DVE", lambda e: e.tensor_tensor(out=yf.t[:, :, 0:Q], in0=yf.t[:, :, 0:Q], in1=szv[:, :, c0:c0 + Q],
                                                      op=ALU.mult), R=(yf.d, sz_d), W=(yf.d,))
                pbN = nb()
                for g in range(4):
                    def sqf(c, g=g):
                        b = nsq()
                        k.op("ACT", lambda e: e.activation(out=b.t[:, 0:Q], in_=yf.t[:, 4 * g + c, 0:Q], func=AF.Square),
                             R=(yf.d,), W=(b.d,))
                        return b.t[:, 0:Q], b.d
                    rms_rstd(sqf, 4, Q, 512.0, rstdg.t[:, g, 0:Q], rstdg.d, pbank=pbN, pslice=pbN.t[:, g * 128:g * 128 + Q])
                yf4 = yf.t[:, :, :].rearrange("p (g j) t -> p g j t", g=4)[:, :, :, 0:Q]
                k.op("DVE", lambda e: e.tensor_tensor(out=yf4, in0=yf4,
                                                      in1=rstdg.t[:, :, 0:Q].unsqueeze(2).broadcast_to([128, 4, 4, Q]),
                                                      op=ALU.mult), R=(yf.d, rstdg.d), W=(yf.d,))
                k.op("DVE", lambda e: e.tensor_tensor(out=xsv[:, :, c0:c0 + Q], in0=yf.t[:, :, 0:Q],
                                                      in1=prm2.t[:, l, 24:40].unsqueeze(2).broadcast_to([128, 16, Q]),
                                                      op=ALU.mult), R=(yf.d, PRM), W=(xd,))

            if STAGE < 4:
                continue
            merged = dbuf
            for b in range(4):
                slot = wload([(w_ssd_out[l, :, 256 * b:256 * (b + 1)].rearrange("(kt p) n -> p kt n", p=128),
                               lambda s: s.t[:, 0:4096].rearrange("p (kt n) -> p kt n", n=256))])
                wv = slot.t[:, 0:4096].rearrange("p (kt n) -> p kt n", n=256)
                for o in range(2):
                    pb = nb()
                    for kt in range(16):
                        k.op("PE", lambda e: e.matmul(pb.t[:, 0:T], wv[:, kt, o * 128:(o + 1) * 128],
                                                      xsv[:, kt, 0:T], start=(kt == 0), stop=(kt == 15)),
                             R=tuple([slot.d] + xs_deps), W=(pb.d,), inc=(kt == 15))
                    ch = 2 * b + o
                    tb = ntmp()
                    k.op("DVE", lambda e: e.tensor_tensor(out=tb.t[:, 0:T], in0=pb.t[:, 0:T], in1=sg.t[:, 8 + ch, 0:T],
                                                          op=ALU.mult), R=(pb.d, sg.d), W=(tb.d,))
                    k.op("DVE", lambda e: e.tensor_tensor(out=merged.t[:, ch, 0:T], in0=tb.t[:, 0:T],
                                                          in1=dbuf.t[:, ch, 0:T], op=ALU.add),
                         R=(tb.d, dbuf.d), W=(merged.d,))
            for b in range(2):
                slot = wload([(w_o[l, :, 512 * b:512 * (b + 1)].rearrange("(kt p) n -> p kt n", p=128),
                               lambda s: s.t[:, 0:4096].rearrange("p (kt n) -> p kt n", n=512))])
                wv = slot.t[:, 0:4096].rearrange("p (kt n) -> p kt n", n=512)
                for o in range(4):
                    pb = nb()
                    for kt in range(8):
                        k.op("PE", lambda e: e.matmul(pb.t[:, 0:T], wv[:, kt, o * 128:(o + 1) * 128],
                                                      merged.t[:, kt, 0:T], start=(kt == 0), stop=(kt == 7)),
                             R=(slot.d, merged.d), W=(pb.d,), inc=(kt == 7))
                    ch = 4 * b + o
                    k.op("DVE", lambda e: e.tensor_tensor(out=x.t[:, ch, 0:T], in0=x.t[:, ch, 0:T], in1=pb.t[:, 0:T],
                                                          op=ALU.add), R=(pb.d, x.d), W=(x.d,))
            if STAGE < 5:
                continue
            rmsnorm_to(hn, T, n2c)
            for jb in range(11):
                slot = wload([(w_up[l, :, 256 * jb:256 * (jb + 1)].rearrange("(kt p) n -> p kt n", p=128),
                               lambda s: s.t[:, 0:4096].rearrange("p (kt n) -> p kt n", n=512)[:, :, 0:256]),
                              (w_up[l, :, DFF + 256 * jb:DFF + 256 * (jb + 1)].rearrange("(kt p) n -> p kt n", p=128),
                               lambda s: s.t[:, 0:4096].rearrange("p (kt n) -> p kt n", n=512)[:, :, 256:512])])
                wv = slot.t[:, 0:4096].rearrange("p (kt n) -> p kt n", n=512)
                for o in range(2):
                    ca = 2 * jb + o
                    pbs = []
                    for half in range(2):
                        pb = nb()
                        for kt in range(8):
                            k.op("PE", lambda e: e.matmul(pb.t[:, 0:T], wv[:, kt, half * 256 + o * 128:half * 256 + (o + 1) * 128],
                                                          hn.t[:, kt, 0:T], start=(kt == 0), stop=(kt == 7)),
                                 R=(slot.d, hn.d), W=(pb.d,), inc=(kt == 7))
                        pbs.append(pb)
                    for seg in tile["segs"]:
                        res = []
                        for half in range(2):
                            chf = ca + 22 * half
                            carry = carry_of(seg, l, "ffn")
                            res.append(conv_pe(pbs[half], seg, 2, carry, chf, [fwc(t_, chf) for t_ in range(3)]))
                        (pcA, pcAv), (pcB, pcBv) = res
                        n, L = seg.nseq, seg.L
                        ta = ntmp()
                        tav = ta.t[:, 0:n * L].rearrange("p (s l) -> p s l", l=L)
                        k.op("ACT", lambda e: e.activation(out=tav, in_=pcAv, func=AF.Silu, bias=fbc(ca), scale=1.0),
                             R=(pcA.d, PRM), W=(ta.d,))
                        k.op("DVE", lambda e: e.scalar_tensor_tensor(out=gv[:, ca, seg.c0:seg.c0 + n * L].rearrange("p (s l) -> p s l", l=L),
                                                                     in0=pcBv, scalar=fbc(ca + 22), in1=tav,
                                                                     op0=ALU.add, op1=ALU.mult),
                             R=(pcB.d, ta.d, PRM), W=R1W)
            for o in range(8):
                slot = wload([(w_down[l, :, 128 * o:128 * (o + 1)].rearrange("(kt p) n -> p kt n", p=128),
                               lambda s: s.t[:, 0:2816].rearrange("p (kt n) -> p kt n", n=128))])
                wv = slot.t[:, 0:2816].rearrange("p (kt n) -> p kt n", n=128)
                pb = nb()
                for kt in range(22):
                    k.op("PE", lambda e: e.matmul(pb.t[:, 0:T], wv[:, kt, :], gv[:, kt, 0:T],
                                                  start=(kt == 0), stop=(kt == 21)),
                         R=(slot.d, sz_d, bc_d), W=(pb.d,), inc=(kt == 21))
                k.op("DVE", lambda e: e.tensor_tensor(out=x.t[:, o, 0:T], in0=x.t[:, o, 0:T], in1=pb.t[:, 0:T],
                                                      op=ALU.add), R=(pb.d, x.d), W=(x.d,))

            if is_ms:
                for half in range(2):
                    store_featmajor_T(lambda ch: sc_pool.t[:, ch, half * 8:(half + 1) * 8, :].rearrange("p s r -> p (s r)"),
                                      sc_pool.d, 120, D, s_pool[l, half * 120:(half + 1) * 120, :])
                store_featmajor_T(lambda ch: sc_conv.t[:, ch, :, :].rearrange("p s r -> p (s r)"), sc_conv.d, 48, CD, s_conv[l])
                store_featmajor_T(lambda ch: sc_ffn.t[:, ch, :, :].rearrange("p s r -> p (s r)"), sc_ffn.d, 32, 2 * DFF, s_ffn[l])
            if last_tile:
                store_featmajor_T(lambda ch: pc_pool[l].t[:, ch, 0, :], pc_pool[l].d, 15, D, p_pool[l])
                store_featmajor_T(lambda ch: pc_conv[l].t[:, ch, 0, :], pc_conv[l].d, 3, CD, p_conv[l])
                store_featmajor_T(lambda ch: pc_ffn[l].t[:, ch, 0, :], pc_ffn[l].d, 2, 2 * DFF, p_ffn[l])

        xnv = R1f[:, 0:8 * T].rearrange("p (c t) -> p c t", c=8)

        def sqf_fin(c):
            b = nsq()
            k.op("ACT", lambda e: e.activation(out=b.t[:, 0:T], in_=x.t[:, c, 0:T], func=AF.Square), R=(x.d,), W=(b.d,))
            return b.t[:, 0:T], b.d
        rms_rstd(sqf_fin, 8, T, float(D), rstd.t[:, 0:T], rstd.d)
        for c in range(8):
            k.op("DVE", lambda e: e.scalar_tensor_tensor(out=xnv[:, c, 0:T], in0=x.t[:, c, 0:T], scalar=fnw.t[:, c:c + 1],
                                                         in1=rstd.t[:, 0:T], op0=ALU.mult, op1=ALU.mult),
                 R=(x.d, rstd.d, PRM), W=R1W)
        if is_ms:
            oblocks = [(NMETA, 128, y_sample)]
        else:
            r0 = tile["idx"] * 512
            oblocks = [(128 * b, 128, y_prompt[r0 + 128 * b:r0 + 128 * (b + 1), :]) for b in range(4)]
        for (c0, n, dst) in oblocks:
            store_featmajor_T(lambda ch: xnv[:, ch, c0:c0 + n], sz_d, n, D, dst)

    k.wait_all("SP", out_dss)
    return nc


_CACHE = {}


def kernel(**inputs):
    if "nc" not in _CACHE:
        _CACHE["nc"] = build_program()
    nc = _CACHE["nc"]
    f = lambda a: np.ascontiguousarray(np.asarray(a, dtype=np.float32))
    shared = {n: f(inputs[n]) for n in ("meta_tokens", "norm1_w", "w_in", "pool_w", "pool_scale", "w_pool_out", "conv_w",
                                         "conv_b", "dt_bias", "a_log", "d_skip", "ssd_norm_w", "w_ssd_out", "w_o",
                                         "norm2_w", "w_up", "ffn_conv_w", "ffn_conv_b", "w_down", "final_norm_w")}
    xp, xs = f(inputs["x_prompt"]), f(inputs["x_sample"])
    sp, sc, sh, sf = f(inputs["state_pool"]), f(inputs["state_conv"]), f(inputs["state_ssm"]), f(inputs["state_ffn"])
    in_maps = []
    for i in range(N_CORES):
        sl = slice(NSEQ * i, NSEQ * (i + 1))
        m = dict(shared)
        m["x_prompt"] = xp[i]
        m["x_sample"] = np.ascontiguousarray(xs[sl].reshape(NSEQ * LS, D))
        m["state_pool"] = np.ascontiguousarray(sp[:, sl].reshape(DEPTH, NSEQ * 15, D))
        m["state_conv"] = np.ascontiguousarray(sc[:, sl].reshape(DEPTH, NSEQ * 3, CD))
        m["state_ssm"] = np.ascontiguousarray(sh[:, sl])
        m["state_ffn"] = np.ascontiguousarray(sf[:, sl].reshape(DEPTH, NSEQ * 2, 2 * DFF))
        in_maps.append(m)
    res = run_bass_kernel_spmd(nc, in_maps, core_ids=list(range(N_CORES)))
    R = res.results
    y_prompt = np.stack([R[i]["y_prompt"] for i in range(N_CORES)], 0)
    y_sample = np.concatenate([R[i]["y_sample"].reshape(NSEQ, LS, D) for i in range(N_CORES)], 0)
    p_pool = np.stack([R[i]["p_pool"] for i in range(N_CORES)], 1)
    p_conv = np.stack([R[i]["p_conv"] for i in range(N_CORES)], 1)
    p_ssm = np.stack([R[i]["p_ssm"] for i in range(N_CORES)], 1)
    p_ffn = np.stack([R[i]["p_ffn"] for i in range(N_CORES)], 1)
    s_pool = np.concatenate([R[i]["s_pool"].reshape(DEPTH, NSEQ, 15, D) for i in range(N_CORES)], 1)
    s_conv = np.concatenate([R[i]["s_conv"].reshape(DEPTH, NSEQ, 3, CD) for i in range(N_CORES)], 1)
    s_ssm = np.concatenate([R[i]["s_ssm"] for i in range(N_CORES)], 1)
    s_ffn = np.concatenate([R[i]["s_ffn"].reshape(DEPTH, NSEQ, 2, 2 * DFF) for i in range(N_CORES)], 1)
    return tuple(np.ascontiguousarray(a.astype(np.float32)) for a in
                 (y_prompt, y_sample, p_pool, p_conv, p_ssm, p_ffn, s_pool, s_conv, s_ssm, s_ffn))
```
